# Optimizing a Trainium2 kernel written in Bass

```python
import math
import jax, jax.numpy as jnp
from jax import lax
import numpy as np

D_MODEL = 2048
BATCH = 4
SEQ = 4096
DEPTH = 4

CTX_LEN = 256
GRID_W = 64
N_MIXERS = 3
CHUNK = 64
CONV_W = 4
EPS = 1e-6
D_FF = 4 * D_MODEL

GDN_QK_HEADS = D_MODEL // 128
GDN_V_HEADS = 2 * GDN_QK_HEADS
GDN_DK = 128
GDN_DV = 128
GDN_QK = GDN_QK_HEADS * GDN_DK
GDN_VW = GDN_V_HEADS * GDN_DV
GDN_REP = GDN_V_HEADS // GDN_QK_HEADS
GDN_IN = 2 * GDN_QK + 2 * GDN_VW + 4 * GDN_V_HEADS

ML_HEADS = 8
ML_DQK = D_MODEL // (2 * ML_HEADS)
ML_DV = D_MODEL // ML_HEADS
ML_QK = ML_HEADS * ML_DQK
ML_V = ML_HEADS * ML_DV
ML_IN = 2 * ML_QK + 2 * ML_V + 4 * ML_HEADS
GATE_CAP = 15.0

LRU_W = D_MODEL
LRU_BLOCKS = 8
LRU_BW = LRU_W // LRU_BLOCKS
LRU_C = 8.0

kernel_name = "hybrid_gdn_mlstm_rglru_prefix_dit"


def rmsnorm(x, w):
    xf = x.astype(jnp.float32)
    y = xf * lax.rsqrt(jnp.mean(xf * xf, axis=-1, keepdims=True) + EPS)
    return (y * w.astype(jnp.float32)).astype(x.dtype)


def modulate(h, shift, scale):
    return h * (1 + scale[:, None, :]) + shift[:, None, :]


def l2norm(t):
    return t * lax.rsqrt(jnp.sum(t * t, axis=-1, keepdims=True) + EPS)


def soft_cap(t):
    return GATE_CAP * jnp.tanh(t / GATE_CAP)


def dwconv(x, w):
    K, C = w.shape
    left = K // 2
    return lax.conv_general_dilated(x, w[:, None, :].astype(x.dtype), window_strides=(1,),
                                    padding=[(left, K - 1 - left)],
                                    dimension_numbers=('NWC', 'WIO', 'NWC'), feature_group_count=C)


def split_heads(t, n):
    B, S, _ = t.shape
    return t.reshape(B, S, n, -1).transpose(0, 2, 1, 3)


def to_chunks(t):
    B, H, S = t.shape[:3]
    t = t.reshape(B, H, S // CHUNK, CHUNK, *t.shape[3:])
    return jnp.moveaxis(t, 2, 0)


def from_chunks(t):
    t = jnp.moveaxis(t, 0, 2)
    return t.reshape(t.shape[0], t.shape[1], -1, *t.shape[4:])


def to_col_major(x):
    B, S, D = x.shape
    rows = S // GRID_W
    return x.reshape(B, rows, GRID_W, D).transpose(0, 2, 1, 3).reshape(B, S, D)


def from_col_major(x):
    B, S, D = x.shape
    rows = S // GRID_W
    return x.reshape(B, GRID_W, rows, D).transpose(0, 2, 1, 3).reshape(B, S, D)


def bidirectional(scan_fn, ctx_f, lat_f, ctx_b, lat_b, init, axis):
    rev = lambda ts: tuple(jnp.flip(t, axis) for t in ts)
    yc_f, st_f = scan_fn(*ctx_f, init)
    yl_f, _ = scan_fn(*lat_f, st_f)
    yc_b, st_b = scan_fn(*rev(ctx_b), init)
    yl_b, _ = scan_fn(*rev(lat_b), st_b)
    return yc_f + jnp.flip(yc_b, axis), yl_f + jnp.flip(yl_b, axis)


def gdn_chunked(q, k, v, g, beta, state0):
    q, k, v, g, beta = map(to_chunks, (q, k, v, g, beta))
    L = CHUNK
    incl = jnp.tril(jnp.ones((L, L), bool))
    strict = jnp.tril(jnp.ones((L, L), bool), -1)
    g = jnp.cumsum(g, axis=-1)
    decay = jnp.exp(jnp.where(incl, g[..., :, None] - g[..., None, :], -jnp.inf))
    kb = k * beta[..., None]
    A = jnp.where(strict, jnp.einsum('nbhid,nbhjd->nbhij', kb, k) * decay, 0.0) + jnp.eye(L, dtype=jnp.float32)
    u = lax.linalg.triangular_solve(A, v * beta[..., None], left_side=True, lower=True, unit_diagonal=True)
    w = lax.linalg.triangular_solve(A, kb * jnp.exp(g)[..., None], left_side=True, lower=True, unit_diagonal=True)
    a_qk = jnp.einsum('nbhid,nbhjd->nbhij', q, k) * decay

    def step(S, xs):
        qc, kc, uc, wc, gc, ac = xs
        v_new = uc - wc @ S
        o = (qc * jnp.exp(gc)[..., None]) @ S + ac @ v_new
        gl = gc[..., -1:]
        S = S * jnp.exp(gl)[..., None] + jnp.einsum('bhld,bhle->bhde', kc * jnp.exp(gl - gc)[..., None], v_new)
        return S, o

    S_fin, o = lax.scan(step, state0, (q, k, u, w, g, a_qk))
    return from_chunks(o), S_fin


def gdn_mixer(h_ctx, h_lat, w_in, conv_w, a_log, dt_bias, norm_w, w_out):
    f32 = jnp.float32

    def prep(h):
        B, S, _ = h.shape
        p = h @ w_in
        qkv = jax.nn.silu(dwconv(p[..., :2 * GDN_QK + GDN_VW], conv_w)).astype(f32)
        z = p[..., 2 * GDN_QK + GDN_VW:2 * GDN_QK + 2 * GDN_VW].reshape(B, S, GDN_V_HEADS, GDN_DV)
        q = jnp.repeat(l2norm(split_heads(qkv[..., :GDN_QK], GDN_QK_HEADS)), GDN_REP, axis=1) * GDN_DK ** -0.5
        k = jnp.repeat(l2norm(split_heads(qkv[..., GDN_QK:2 * GDN_QK], GDN_QK_HEADS)), GDN_REP, axis=1)
        v = split_heads(qkv[..., 2 * GDN_QK:], GDN_V_HEADS)
        ba = p[..., 2 * GDN_QK + 2 * GDN_VW:].astype(f32).reshape(B, S, 2, 2, GDN_V_HEADS)
        beta = jax.nn.sigmoid(ba[:, :, 0]).transpose(0, 2, 3, 1)
        g = -jnp.exp(a_log.astype(f32))[:, :, None] * jax.nn.softplus(
            ba[:, :, 1].transpose(0, 2, 3, 1) + dt_bias.astype(f32)[:, :, None])
        return q, k, v, beta, g, z

    qc, kc, vc, bc, gc, zc = prep(h_ctx)
    ql, kl, vl, bl, gl, zl = prep(h_lat)
    init = jnp.zeros((h_ctx.shape[0], GDN_V_HEADS, GDN_DK, GDN_DV), f32)
    oc, ol = bidirectional(gdn_chunked,
                           (qc, kc, vc, gc[:, 0], bc[:, 0]), (ql, kl, vl, gl[:, 0], bl[:, 0]),
                           (qc, kc, vc, gc[:, 1], bc[:, 1]), (ql, kl, vl, gl[:, 1], bl[:, 1]),
                           init, axis=2)

    def out(o, z, h):
        B, S, _ = h.shape
        y = rmsnorm(o.transpose(0, 2, 1, 3), norm_w) * jax.nn.silu(z.astype(f32))
        return y.reshape(B, S, GDN_VW).astype(h.dtype) @ w_out

    return out(oc, zc, h_ctx), out(ol, zl, h_lat)


def mlstm_chunked(q, k, v, ig, lf, state):
    L = CHUNK
    causal = jnp.tril(jnp.ones((L, L), bool))

    def step(carry, xs):
        C, n, m = carry
        qc, kc, vc, ic, fc = xs
        b = jnp.cumsum(fc, axis=-1)
        d_log = jnp.where(causal, b[..., :, None] - b[..., None, :] + ic[..., None, :], -jnp.inf)
        inter = b + m[..., None]
        m_t = jnp.maximum(inter, jnp.max(d_log, axis=-1))
        s = jnp.einsum('bhld,bhsd->bhls', qc, kc) * jnp.exp(d_log - m_t[..., None])
        w_inter = jnp.exp(inter - m_t)
        num = w_inter[..., None] * jnp.einsum('bhld,bhde->bhle', qc, C) + jnp.einsum('bhls,bhse->bhle', s, vc)
        den = w_inter * jnp.einsum('bhld,bhd->bhl', qc, n) + jnp.sum(s, axis=-1)
        h = num / jnp.maximum(jnp.abs(den), jnp.exp(-m_t))[..., None]
        b_last = b[..., -1]
        w_log = b_last[..., None] - b + ic
        m_new = jnp.maximum(b_last + m, jnp.max(w_log, axis=-1))
        w_k = jnp.exp(w_log - m_new[..., None])
        carry_decay = jnp.exp(b_last + m - m_new)
        C = carry_decay[..., None, None] * C + jnp.einsum('bhl,bhld,bhle->bhde', w_k, kc, vc)
        n = carry_decay[..., None] * n + jnp.einsum('bhl,bhld->bhd', w_k, kc)
        return (C, n, m_new), h

    state, h = lax.scan(step, state, tuple(map(to_chunks, (q, k, v, ig, lf))))
    return from_chunks(h), state


def mlstm_mixer(h_ctx, h_lat, w_in, gate_b, norm_w, w_out):
    f32 = jnp.float32

    def prep(h):
        B, S, _ = h.shape
        p = h @ w_in
        q = split_heads(p[..., :ML_QK], ML_HEADS).astype(f32) * ML_DQK ** -0.5
        k = split_heads(p[..., ML_QK:2 * ML_QK], ML_HEADS).astype(f32)
        v = split_heads(p[..., 2 * ML_QK:2 * ML_QK + ML_V], ML_HEADS).astype(f32)
        o = p[..., 2 * ML_QK + ML_V:2 * ML_QK + 2 * ML_V].reshape(B, S, ML_HEADS, ML_DV)
        gt = soft_cap(p[..., 2 * ML_QK + 2 * ML_V:].astype(f32).reshape(B, S, 2, 2, ML_HEADS)
                      + gate_b.astype(f32)).transpose(0, 2, 3, 4, 1)
        ig = gt[:, :, 0]
        lf = jax.nn.log_sigmoid(gt[:, :, 1])
        return q, k, v, ig, lf, o

    qc, kc, vc, ic, fc, oc = prep(h_ctx)
    ql, kl, vl, il, fl, ol = prep(h_lat)
    B = h_ctx.shape[0]
    init = (jnp.zeros((B, ML_HEADS, ML_DQK, ML_DV), f32), jnp.zeros((B, ML_HEADS, ML_DQK), f32),
            jnp.zeros((B, ML_HEADS), f32))
    hc, hl = bidirectional(mlstm_chunked,
                           (qc, kc, vc, ic[:, 0], fc[:, 0]), (ql, kl, vl, il[:, 0], fl[:, 0]),
                           (qc, kc, vc, ic[:, 1], fc[:, 1]), (ql, kl, vl, il[:, 1], fl[:, 1]),
                           init, axis=2)

    def out(hh, o, h):
        B_, S, _ = h.shape
        y = rmsnorm(hh.transpose(0, 2, 1, 3), norm_w.reshape(ML_HEADS, ML_DV)) * jax.nn.sigmoid(o.astype(f32))
        return y.reshape(B_, S, ML_V).astype(h.dtype) @ w_out

    return out(hc, oc, h_ctx), out(hl, ol, h_lat)


def lru_scan(a, u, h0):
    def combine(left, right):
        return left[0] * right[0], right[0] * left[1] + right[1]
    a_cum, h = lax.associative_scan(combine, (a, u), axis=1)
    h = h + a_cum * h0[:, None, :]
    return h, h[:, -1]


def lru_mixer(h_ctx, h_lat, w_in, conv_w, conv_b, w_gate, b_gate, lam, w_out):
    f32 = jnp.float32

    def prep(h):
        B, S, _ = h.shape
        p = h @ w_in
        y = jax.nn.gelu(p[..., :LRU_W])
        xr = (dwconv(p[..., LRU_W:], conv_w) + conv_b).astype(f32)
        gt = jnp.einsum('bsnk,dgnkj->dgbsnj', xr.reshape(B, S, LRU_BLOCKS, LRU_BW), w_gate.astype(f32))
        gt = gt.reshape(2, 2, B, S, LRU_W) + b_gate.astype(f32)[:, :, None, None, :]
        log_a = -LRU_C * jax.nn.sigmoid(gt[:, 0]) * jax.nn.softplus(-lam.astype(f32))[:, None, None, :]
        a = jnp.exp(log_a)
        u = jnp.sqrt(-jnp.expm1(2.0 * log_a)) * jax.nn.sigmoid(gt[:, 1]) * xr[None]
        return y, a, u

    yc, ac, uc = prep(h_ctx)
    yl, al, ul = prep(h_lat)
    init = jnp.zeros((h_ctx.shape[0], LRU_W), f32)
    hc, hl = bidirectional(lru_scan, (ac[0], uc[0]), (al[0], ul[0]), (ac[1], uc[1]), (al[1], ul[1]), init, axis=1)
    return (hc * yc).astype(h_ctx.dtype) @ w_out, (hl * yl).astype(h_lat.dtype) @ w_out


def sqrelu_mlp(h, w_up, w_down):
    return jnp.square(jax.nn.relu(h @ w_up)) @ w_down


def setup_inputs(seed: int = 0) -> dict:
    key = jax.random.key(seed)
    ks = iter(jax.random.split(key, 40))
    f32 = jnp.float32
    nrm = lambda shape, scale: jax.random.normal(next(ks), shape, f32) * scale
    uni = lambda shape, lo, hi: jax.random.uniform(next(ks), shape, f32, lo, hi)
    D = D_MODEL
    nA = len(range(0, DEPTH, N_MIXERS))
    nB = len(range(1, DEPTH, N_MIXERS))
    nC = len(range(2, DEPTH, N_MIXERS))
    inp = {}
    inp["x"] = nrm((BATCH, SEQ, D), 1.0)
    inp["c"] = nrm((BATCH, D), 1.0)
    inp["ctx"] = nrm((BATCH, CTX_LEN, D), 1.0)
    inp["c_ctx"] = nrm((D,), 1.0)
    inp["w_mod"] = nrm((DEPTH, D, 6 * D), 0.5 * D ** -0.5)
    inp["b_mod"] = nrm((DEPTH, 6 * D), 0.02)
    inp["norm_mix"] = 1.0 + nrm((DEPTH, D), 0.05)
    inp["norm_ff"] = 1.0 + nrm((DEPTH, D), 0.05)
    inp["norm_out"] = 1.0 + nrm((D,), 0.05)
    inp["gdn_w_in"] = nrm((nA, D, GDN_IN), D ** -0.5)
    inp["gdn_conv"] = nrm((nA, CONV_W, 2 * GDN_QK + GDN_VW), CONV_W ** -0.5)
    inp["gdn_a_log"] = jnp.log(uni((nA, 2, GDN_V_HEADS), 1.0, 16.0))
    dt = jnp.exp(uni((nA, 2, GDN_V_HEADS), math.log(1e-3), math.log(1e-1)))
    inp["gdn_dt_bias"] = dt + jnp.log(-jnp.expm1(-dt))
    inp["gdn_norm"] = 1.0 + nrm((nA, GDN_DV), 0.05)
    inp["gdn_w_out"] = nrm((nA, GDN_VW, D), GDN_VW ** -0.5)
    inp["ml_w_in"] = nrm((nB, D, ML_IN), D ** -0.5)
    i_bias = nrm((nB, 2, ML_HEADS), 0.5)
    f_bias = uni((nB, 2, ML_HEADS), 3.0, 6.0)
    inp["ml_gate_b"] = jnp.stack([i_bias, f_bias], axis=2)
    inp["ml_norm"] = 1.0 + nrm((nB, ML_V), 0.05)
    inp["ml_w_out"] = nrm((nB, ML_V, D), ML_V ** -0.5)
    inp["lru_w_in"] = nrm((nC, D, 2 * LRU_W), D ** -0.5)
    inp["lru_conv"] = nrm((nC, CONV_W, LRU_W), CONV_W ** -0.5)
    inp["lru_conv_b"] = nrm((nC, LRU_W), 0.02)
    inp["lru_w_gate"] = nrm((nC, 2, 2, LRU_BLOCKS, LRU_BW, LRU_BW), LRU_BW ** -0.5)
    inp["lru_b_gate"] = nrm((nC, 2, 2, LRU_W), 0.02)
    s = uni((nC, 2, LRU_W), 0.9, 0.999) ** (1.0 / LRU_C)
    inp["lru_lambda"] = jnp.log(s) - jnp.log1p(-s)
    inp["lru_w_out"] = nrm((nC, LRU_W, D), LRU_W ** -0.5)
    inp["ff_w_up"] = nrm((DEPTH, D, D_FF), D ** -0.5)
    inp["ff_w_down"] = nrm((DEPTH, D_FF, D), D_FF ** -0.5)
    return inp


def reference(x, c, ctx, c_ctx, w_mod, b_mod, norm_mix, norm_ff, norm_out,
              gdn_w_in, gdn_conv, gdn_a_log, gdn_dt_bias, gdn_norm, gdn_w_out,
              ml_w_in, ml_gate_b, ml_norm, ml_w_out,
              lru_w_in, lru_conv, lru_conv_b, lru_w_gate, lru_b_gate, lru_lambda, lru_w_out,
              ff_w_up, ff_w_down):
    sc_lat = jax.nn.silu(c)
    sc_ctx = jax.nn.silu(c_ctx)[None]
    lat, cx = x, ctx
    for i in range(DEPTH):
        kind, j = i % N_MIXERS, i // N_MIXERS
        sh1_l, sc1_l, g1_l, sh2_l, sc2_l, g2_l = jnp.split(sc_lat @ w_mod[i] + b_mod[i], 6, axis=-1)
        sh1_c, sc1_c, g1_c, sh2_c, sc2_c, g2_c = jnp.split(sc_ctx @ w_mod[i] + b_mod[i], 6, axis=-1)
        hc = modulate(rmsnorm(cx, norm_mix[i]), sh1_c, sc1_c)
        hl = modulate(rmsnorm(lat, norm_mix[i]), sh1_l, sc1_l)
        col = (i % 2 == 1)
        if col:
            hl = to_col_major(hl)
        if kind == 0:
            yc, yl = gdn_mixer(hc, hl, gdn_w_in[j], gdn_conv[j], gdn_a_log[j], gdn_dt_bias[j],
                               gdn_norm[j], gdn_w_out[j])
        elif kind == 1:
            yc, yl = mlstm_mixer(hc, hl, ml_w_in[j], ml_gate_b[j], ml_norm[j], ml_w_out[j])
        else:
            yc, yl = lru_mixer(hc, hl, lru_w_in[j], lru_conv[j], lru_conv_b[j], lru_w_gate[j],
                               lru_b_gate[j], lru_lambda[j], lru_w_out[j])
        if col:
            yl = from_col_major(yl)
        lat = lat + g1_l[:, None, :] * yl
        lat = lat + g2_l[:, None, :] * sqrelu_mlp(modulate(rmsnorm(lat, norm_ff[i]), sh2_l, sc2_l),
                                                  ff_w_up[i], ff_w_down[i])
        if i < DEPTH - 1:
            cx = cx + g1_c[:, None, :] * yc
            cx = cx + g2_c[:, None, :] * sqrelu_mlp(modulate(rmsnorm(cx, norm_ff[i]), sh2_c, sc2_c),
                                                    ff_w_up[i], ff_w_down[i])
    return rmsnorm(lat, norm_out)
```

```python
import numpy as np
from contextlib import ExitStack
import concourse.bass as bass
import concourse.mybir as mybir
from concourse.bass_utils import run_bass_kernel_spmd

F32 = mybir.dt.float32
BF16 = mybir.dt.bfloat16
AF = mybir.ActivationFunctionType
ALU = mybir.AluOpType

D = 2048
NCTX = 256
NLAT = 4096
S = NCTX + NLAT
DFF = 8192
KC = D // 128
DEPTH = 4
EPS = 1e-6
TOK_TILES = [(0, 256)] + [(256 + 512 * k, 512) for k in range(8)]
TOK256 = [(256 * k, 256) for k in range(17)]
GDN_IN = 12416
ML_IN = 6176
LRU_IN = 4096

import os as _osg
RAW_ONLY = _osg.environ.get("RAW_ONLY", "0") == "1"
COMPUTE = ("pe", "act", "dve", "pool")
ALLQ = ("pe", "act", "dve", "pool", "sp")


class Prog:
    def __init__(self, nc, n_dma_sems=24):
        self.nc = nc
        self.es = ExitStack()
        self.scopes = []
        self.q = {e: [] for e in ALLQ}
        self.cnt = {e: 0 for e in ALLQ}
        self.sem = {}
        for e in COMPUTE:
            self.sem[e] = self.es.enter_context(nc.semaphore("prog_" + e))
        self.dma_pool = {}
        self.dma_idx = {}
        for qn in ("sp", "act", "pool"):
            self.dma_pool[qn] = [self.es.enter_context(nc.semaphore(f"dma_{qn}_{i}")) for i in range(n_dma_sems)]
            self.dma_idx[qn] = 0
        self.waited = {e: {} for e in ALLQ}
        self.state = {}
        self.n_tiles = 0

    def _es(self):
        return self.scopes[-1] if self.scopes else self.es

    def push(self):
        self.scopes.append(ExitStack())

    def pop(self):
        self.barrier()
        self.scopes.pop().close()

    def sbuf(self, name, shape, dtype=F32):
        self.n_tiles += 1
        return self._es().enter_context(self.nc.sbuf_tensor(f"{name}_{self.n_tiles}", list(shape), dtype))

    def psum(self, name, shape, dtype=F32):
        self.n_tiles += 1
        return self._es().enter_context(self.nc.psum_tensor(f"{name}_{self.n_tiles}", list(shape), dtype))

    def dram(self, name, shape, dtype=F32):
        return self.nc.dram_tensor(name, list(shape), dtype, kind="Internal").ap()

    def _deps(self, eng, reads, writes):
        deps = []
        for k in reads:
            st = self.state.get(k)
            if st and st["w"] is not None:
                deps.append((st["w"], "raw"))
        for k in writes:
            st = self.state.get(k)
            if st:
                if st["w"] is not None:
                    deps.append((st["w"], "waw"))
                for t in st["r"].values():
                    deps.append((t, "war"))
        need = {}
        for (tok, kind) in deps:
            semkey, semh, val, own = tok
            if own == eng and own in COMPUTE:
                if eng == "pe" or (kind != "raw" and RAW_ONLY):
                    continue
            if self.waited[eng].get(semkey, 0) >= val:
                continue
            if semkey not in need or need[semkey][1] < val:
                need[semkey] = (semh, val)
        for semkey, (semh, val) in need.items():
            self.waited[eng][semkey] = val
        return list(need.values())

    def _update(self, tok, reads, writes):
        for k in writes:
            self.state[k] = {"w": tok, "r": {}}
        for k in reads:
            if k in writes:
                continue
            st = self.state.setdefault(k, {"w": None, "r": {}})
            st["r"][tok[0]] = tok

    def op(self, eng, fn, reads=(), writes=()):
        writes = list(writes) + [k for k in reads if isinstance(k, str) and k.startswith("ps") and k not in writes]
        waits = self._deps(eng, reads, writes)
        self.cnt[eng] += 1
        tok = (eng, self.sem[eng], self.cnt[eng], eng)
        self.q[eng].append((waits, fn, (self.sem[eng], 1)))
        self._update(tok, reads, writes)
        return tok

    def dma(self, qn, out, in_, reads=(), writes=(), **kw):
        waits = self._deps(qn, reads, writes)
        pool = self.dma_pool[qn]
        i = self.dma_idx[qn]
        self.dma_idx[qn] += 1
        semh = pool[i % len(pool)]
        uses = i // len(pool)
        semkey = (qn, i % len(pool))
        if uses > 0 and self.waited[qn].get(semkey, 0) < 16 * uses:
            waits.append((semh, 16 * uses))
            self.waited[qn][semkey] = 16 * uses
        tok = (semkey, semh, 16 * (uses + 1), "dma")

        def fn(e, out=out, in_=in_, kw=kw):
            return e.dma_start(out=out, in_=in_, **kw)

        self.q[qn].append((waits, fn, (semh, 16)))
        self._update(tok, reads, writes)
        return tok

    def _all_tokens(self):
        toks = []
        for e in COMPUTE:
            if self.cnt[e] > 0:
                toks.append((e, self.sem[e], self.cnt[e]))
        for qn in self.dma_pool:
            pool = self.dma_pool[qn]
            n = self.dma_idx[qn]
            for j in range(min(n, len(pool))):
                uses = (n - 1 - j) // len(pool) + 1
                toks.append(((qn, j), pool[j], 16 * uses))
        return toks

    def barrier(self):
        toks = self._all_tokens()
        for e in ALLQ:
            waits = []
            for (semkey, semh, val) in toks:
                if semkey == e:
                    continue
                if self.waited[e].get(semkey, 0) >= val:
                    continue
                self.waited[e][semkey] = val
                waits.append((semh, val))
            if waits:
                self.q[e].append((waits, None, None))
        self.state = {}

    def finish(self):
        nc = self.nc
        self.barrier()
        q = self.q

        def replay(e, items):
            for (waits, fn, inc) in items:
                for (semh, val) in waits:
                    e.wait_ge(semh, val)
                if fn is not None:
                    ins = fn(e)
                    ins.then_inc(inc[0], inc[1])

        with nc.Block() as block:
            @block.tensor
            def _(e):
                replay(e, q["pe"])

            @block.scalar
            def _(e):
                replay(e, q["act"])

            @block.vector
            def _(e):
                replay(e, q["dve"])

            @block.gpsimd
            def _(e):
                replay(e, q["pool"])

            @block.sync
            def _(e):
                replay(e, q["sp"])
        while self.scopes:
            self.scopes.pop().close()
        self.es.close()
        return nc


def const_layout():
    cols = {}
    off = 0

    def add(name, n):
        nonlocal off
        cols[name] = off
        off += n

    add("c", 32)
    add("bmod", 4 * 96)
    add("nmix", 4 * 16)
    add("nff", 4 * 16)
    add("nout", 16)
    add("lconv", 4 * 16)
    add("lconvb", 16)
    add("lbg", 4 * 16)
    add("llam", 2 * 16)
    add("gconv", 2 * 4 * 64)
    add("gnorm", 2)
    add("galog", 2)
    add("gdt", 2)
    add("mnorm", 16)
    add("mgb", 1)
    add("ident", 128)
    add("mask", 4 * 128)
    add("bmask", 9 * 128)
    return cols, off


def fm(v):
    v = np.asarray(v, np.float32)
    lead = v.shape[:-1]
    n = v.shape[-1] // 128
    v = v.reshape(*lead, n, 128)
    v = np.moveaxis(v, -1, 0)
    return np.ascontiguousarray(v.reshape(128, -1))


def make_consts(inp, b):
    cols, ncol = const_layout()
    C = np.zeros((128, ncol), np.float32)

    def put(name, arr):
        arr = np.asarray(arr, np.float32)
        C[: arr.shape[0], cols[name]: cols[name] + arr.shape[1]] = arr

    cc = np.stack([inp["c"][b], inp["c_ctx"]], axis=-1)
    put("c", cc.reshape(16, 128, 2).transpose(1, 0, 2).reshape(128, 32))
    put("bmod", fm(inp["b_mod"]))
    put("nmix", fm(inp["norm_mix"]))
    put("nff", fm(inp["norm_ff"]))
    put("nout", fm(inp["norm_out"]))
    put("lconv", fm(inp["lru_conv"][0]))
    put("lconvb", fm(inp["lru_conv_b"][0]))
    put("lbg", fm(inp["lru_b_gate"][0]))
    put("llam", fm(inp["lru_lambda"][0]))
    put("gconv", fm(inp["gdn_conv"]))
    put("gnorm", np.asarray(inp["gdn_norm"]).T)
    put("galog", np.asarray(inp["gdn_a_log"]).reshape(2, 64).T)
    put("gdt", np.asarray(inp["gdn_dt_bias"]).reshape(2, 64).T)
    put("mnorm", fm(inp["ml_norm"][0]))
    put("mgb", np.asarray(inp["ml_gate_b"][0]).reshape(32, 1))
    put("ident", np.eye(128, dtype=np.float32))
    jj = np.arange(128)[:, None]
    ii = np.arange(128)[None, :]
    NEG = -30000.0
    ms = [np.where(jj <= ii, 0.0, NEG), np.where(jj < ii, 0.0, NEG), np.where(jj >= ii, 0.0, NEG), np.where(jj > ii, 0.0, NEG)]
    put("mask", np.concatenate(ms, axis=1))
    bm = [(jj // 8 == ii // 8).astype(np.float32)]
    for s_ in (8, 16, 32, 64):
        bm.append((((jj // s_) % 2 == 1) & ((ii // s_) == (jj // s_) - 1)).astype(np.float32))
    for s_ in (8, 16, 32, 64):
        bm.append(bm[1 + (8, 16, 32, 64).index(s_)].T.copy())
    put("bmask", np.concatenate(bm, axis=1))
    return C


class Builder:
    def __init__(self, stage="full", layers=(0, 1, 2, 3)):
        self.stage = stage
        self.layers = layers
        nc = bass.Bass("TRN2", target_bir_lowering=False)
        self.nc = nc
        self.P = Prog(nc)
        self.cols, self.ncol = const_layout()

    def ext_in(self, name, shape, dtype=F32):
        return self.nc.dram_tensor(name, list(shape), dtype, kind="ExternalInput").ap()

    def ext_out(self, name, shape, dtype=F32):
        return self.nc.dram_tensor(name, list(shape), dtype, kind="ExternalOutput").ap()

    def setup(self):
        P = self.P
        self.consts_d = self.ext_in("consts", [128, self.ncol])
        self.CS = P.sbuf("consts", [128, self.ncol])
        P.dma("sp", self.CS[:], self.consts_d, writes=["consts"])
        self.PS = [P.psum("ps", [128, 512]) for _ in range(6)]
        self.ps_i = 0
        self.ones_bf = P.sbuf("ones_bf", [128, 128], BF16)
        P.op("dve", lambda e: e.memset(self.ones_bf[:], 1.0), writes=["ones_bf"])
        self.modT = P.sbuf("modT", [128, 4, 96, 2])
        self.Amod = P.sbuf("Amod", [128, 4, 2, 16, 2])
        self.zero_col = P.sbuf("zero_col", [128, 1])
        P.op("dve", lambda e: e.memset(self.zero_col[:], 0.0), writes=["zero_col"])
        self.alt = 0
        self.eps_col = P.sbuf("eps_col", [128, 1])
        P.op("dve", lambda e: e.memset(self.eps_col[:], EPS), writes=["eps"])
        self.one_col = P.sbuf("one_col", [128, 1])
        P.op("dve", lambda e: e.memset(self.one_col[:], 1.0), writes=["one_col"])

    def cs(self, name, idx=0, n=1):
        o = self.cols[name] + idx
        return self.CS[:, o:o + n]

    def next_ps(self):
        i = self.ps_i % len(self.PS)
        self.ps_i += 1
        return self.PS[i], f"ps{i}"

    def alt_eng(self):
        self.alt += 1
        return "act" if self.alt % 2 else "dve"

    def evac_copy(self, eng, out, in_, reads, writes):
        P = self.P
        if eng == "act":
            P.op("act", lambda e: e.activation(out=out, in_=in_, func=AF.Copy), reads=reads, writes=writes)
        else:
            P.op("dve", lambda e: e.tensor_copy(out=out, in_=in_), reads=reads, writes=writes)

    def modulation(self, w_mod):
        P = self.P
        P.push()
        scT = P.sbuf("scT", [128, 32])
        P.op("act", lambda e: e.activation(out=scT[:], in_=self.cs("c", 0, 32), func=AF.Silu),
             reads=["consts"], writes=["scT"])
        wb = [P.sbuf("wmod", [128, 16, 512]) for _ in range(2)]
        n = 0
        for i in range(DEPTH):
            wv = w_mod[i].rearrange("(k p) f -> p k f", p=128)
            for g in range(24):
                wt = wb[n % 2]
                wk = f"wmod{n % 2}"
                n += 1
                P.dma("sp", wt[:], wv[:, :, g * 512:(g + 1) * 512], writes=[wk])
                ps, pk = self.next_ps()
                for j in range(4):
                    for k in range(16):
                        P.op("pe", lambda e, ps=ps, wt=wt, j=j, k=k: e.matmul(
                            ps[:, j * 2:(j + 1) * 2], lhsT=wt[:, k, j * 128:(j + 1) * 128],
                            rhs=scT[:, k * 2:(k + 1) * 2], start=(k == 0), stop=(k == 15)),
                            reads=[wk, "scT"], writes=[pk])
                for j in range(4):
                    f = g * 4 + j
                    P.op("dve", lambda e, ps=ps, j=j, f=f, i=i: e.tensor_scalar(
                        out=self.modT[:, i, f, :], in0=ps[:, j * 2:(j + 1) * 2],
                        scalar1=self.cs("bmod", i * 96 + f), scalar2=None, op0=ALU.add),
                        reads=[pk, "consts"], writes=["modT"])
        for i in range(DEPTH):
            for sub in range(2):
                nm = "nmix" if sub == 0 else "nff"
                sc_split = 1 if sub == 0 else 4
                for k in range(16):
                    P.op("dve", lambda e, i=i, sub=sub, k=k, nm=nm, sc_split=sc_split: e.tensor_scalar(
                        out=self.Amod[:, i, sub, k, :], in0=self.modT[:, i, sc_split * 16 + k, :],
                        scalar1=1.0, scalar2=self.cs(nm, i * 16 + k), op0=ALU.add, op1=ALU.mult),
                        reads=["modT", "consts"], writes=["Amod"])
        P.pop()

    def mod_ap(self, i, split, k, s):
        return self.modT[:, i, split * 16 + k, s:s + 1]

    def norm_mod(self, src, A_fn, B_fn, hT=None, dst=None, dst_off=0):
        P = self.P
        P.push()
        xb = [P.sbuf("xa", [128, 16, 256]) for _ in range(2)]
        sqb = [P.sbuf("sq", [128, 16, 256], BF16) for _ in range(2)]
        rsb = [P.sbuf("rstd", [128, 256]) for _ in range(2)]
        tmpb = [P.sbuf("tmpa", [128, 256]) for _ in range(3)]
        ob = [P.sbuf("oa", [128, 16, 256]) for _ in range(2)] if dst is not None else None
        srcv = src.rearrange("(k p) t -> p k t", p=128)
        for ti, (t0, tl) in enumerate(TOK256):
            s = 1 if t0 < NCTX else 0
            if dst is not None and t0 < dst_off:
                continue
            xt, xk = xb[ti % 2], f"xa{ti % 2}"
            sq, sk = sqb[ti % 2], f"sq{ti % 2}"
            rs, rk = rsb[ti % 2], f"rs{ti % 2}"
            P.dma("sp", xt[:], srcv[:, :, t0:t0 + tl], writes=[xk])
            P.op("act", lambda e, sq=sq, xt=xt: e.activation(out=sq[:], in_=xt[:], func=AF.Square),
                 reads=[xk], writes=[sk])
            ps, pk = self.next_ps()
            for k in range(16):
                P.op("pe", lambda e, ps=ps, sq=sq, k=k: e.matmul(ps[:, 0:256], lhsT=self.ones_bf[:], rhs=sq[:, k, :],
                                                                start=(k == 0), stop=(k == 15)),
                     reads=[sk, "ones_bf"], writes=[pk])
            P.op("act", lambda e, rs=rs, ps=ps: e.activation(out=rs[:], in_=ps[:, 0:256], func=AF.Ln, scale=1.0 / D, bias=EPS_AP(self)),
                 reads=[pk, "eps"], writes=[rk])
            P.op("act", lambda e, rs=rs: e.activation(out=rs[:], in_=rs[:], func=AF.Exp, scale=-0.5),
                 reads=[rk], writes=[rk])
            for k in range(16):
                tmp, tk = tmpb[k % 3], f"tmpa{k % 3}"
                P.op("dve", lambda e, tmp=tmp, xt=xt, k=k, s=s, rs=rs: e.scalar_tensor_tensor(
                    out=tmp[:], in0=xt[:, k, :], scalar=A_fn(k, s), in1=rs[:], op0=ALU.mult, op1=ALU.mult),
                    reads=[xk, rk, "Amod", "consts"], writes=[tk])
                if hT is not None:
                    o_ap = hT[:, k, t0:t0 + tl]
                    ok = ("hT", k, t0 // 256)
                else:
                    o_ap = ob[ti % 2][:, k, :]
                    ok = f"oa{ti % 2}"
                P.op("act", lambda e, o_ap=o_ap, tmp=tmp, k=k, s=s: e.activation(
                    out=o_ap, in_=tmp[:], func=AF.Identity, bias=B_fn(k, s), scale=1.0),
                    reads=[tk, "modT", "zero_col"], writes=[ok])
            if dst is not None:
                dv = dst.rearrange("(k p) t -> p k t", p=128)
                P.dma("sp", dv[:, :, t0 - dst_off:t0 - dst_off + tl], ob[ti % 2][:], reads=[f"oa{ti % 2}"])
        P.pop()

    def hT_keys(self, k, t0, tl):
        return [("hT", k, j) for j in range(t0 // 256, (t0 + tl) // 256)]

    def linear_hT(self, w, F, hT, evac, row_done):
        P = self.P
        wb = [P.sbuf("wlin", [128, 16, 128], BF16) for _ in range(3)]
        wv = w.rearrange("(k p) f -> p k f", p=128)
        nj = (F + 127) // 128
        for j in range(nj):
            M = min(128, F - j * 128)
            wt, wk = wb[j % 3], f"wlin{j % 3}"
            P.dma("pool", wt[:, :, 0:M], wv[:, :, j * 128:j * 128 + M], writes=[wk])
            for ti, (t0, tl) in enumerate(TOK_TILES):
                ps, pk = self.next_ps()
                for k in range(16):
                    P.op("pe", lambda e, ps=ps, wt=wt, k=k, M=M, t0=t0, tl=tl: e.matmul(
                        ps[0:M, 0:tl], lhsT=wt[:, k, 0:M], rhs=hT[:, k, t0:t0 + tl], start=(k == 0), stop=(k == 15)),
                        reads=[wk] + self.hT_keys(k, t0, tl), writes=[pk])
                evac(j, M, ti, t0, tl, ps, pk)
            row_done(j, M)

    def in_proj(self, w, F, hT, pT, col):
        P = self.P
        stg = [P.sbuf("stg", [128, S]) for _ in range(2)]

        def evac(j, M, ti, t0, tl, ps, pk):
            st = stg[j % 2]
            if col and t0 >= NCTX:
                r0 = (t0 - NCTX) // 64
                o_ap = st[0:M, NCTX:].rearrange("p (w r) -> p r w", r=64)[:, r0:r0 + 8, :]
                i_ap = ps[0:M, 0:512].rearrange("p (r w) -> p r w", w=64)
            else:
                o_ap = st[0:M, t0:t0 + tl]
                i_ap = ps[0:M, 0:tl]
            self.evac_copy(self.alt_eng(), o_ap, i_ap, [pk], [(f"stg{j % 2}", ti)])

        def row_done(j, M):
            st = stg[j % 2]
            P.dma("sp", pT[j * 128:j * 128 + M, :], st[0:M, :], reads=[(f"stg{j % 2}", ti) for ti in range(9)])
            for ti in range(9):
                pass

        self.linear_hT(w, F, hT, evac, row_done)

    def out_proj(self, yT, w, Kdim, G_fn, res, dump=None):
        P = self.P
        P.push()
        kc = Kdim // 128
        W = P.sbuf("wout", [128, kc, D], BF16)
        wv = w.rearrange("(k p) d -> p k d", p=128)
        for k0 in range(0, kc, 4):
            P.dma("pool", W[:, k0:k0 + 4, :], wv[:, k0:k0 + 4, :], writes=[("wout", k0)])
        yb = [P.sbuf("yb", [128, kc, 256], BF16) for _ in range(2)]
        nrb = 1 if kc > 16 else 2
        rb = [P.sbuf("rb", [128, 16, 256]) for _ in range(nrb)]
        yv = yT.rearrange("(k p) t -> p k t", p=128)
        rv = res.rearrange("(k p) t -> p k t", p=128) if res is not None else None
        dvv = dump.rearrange("(k p) t -> p k t", p=128) if dump is not None else None
        for ti, (t0, tl) in enumerate(TOK256):
            s = 1 if t0 < NCTX else 0
            yt, yk = yb[ti % 2], f"yb{ti % 2}"
            rt, rk = rb[ti % nrb], f"rb{ti % nrb}"
            P.dma("sp", yt[:], yv[:, :, t0:t0 + tl], writes=[yk])
            if dump is None:
                P.dma("sp", rt[:], rv[:, :, t0:t0 + tl], writes=[rk])
            for d in range(16):
                ps, pk = self.next_ps()
                for k in range(kc):
                    P.op("pe", lambda e, ps=ps, k=k, d=d, yt=yt: e.matmul(
                        ps[:, 0:256], lhsT=W[:, k, d * 128:(d + 1) * 128], rhs=yt[:, k, :], start=(k == 0), stop=(k == kc - 1)),
                        reads=[("wout", (k // 4) * 4), yk], writes=[pk])
                if dump is None:
                    P.op("dve", lambda e, ps=ps, rt=rt, d=d, s=s: e.scalar_tensor_tensor(
                        out=rt[:, d, :], in0=ps[:, 0:256], scalar=G_fn(d, s), in1=rt[:, d, :], op0=ALU.mult, op1=ALU.add),
                        reads=[pk, rk, "modT"], writes=[rk])
                else:
                    self.evac_copy(self.alt_eng(), rt[:, d, :], ps[:, 0:256], [pk], [rk])
            if dump is None:
                P.dma("sp", rv[:, :, t0:t0 + tl], rt[:], reads=[rk])
            else:
                P.dma("sp", dvv[:, :, t0:t0 + tl], rt[:], reads=[rk])
        P.pop()

    def ffn(self, i, w_up, w_down, res, aT):
        P = self.P
        P.push()
        hT = P.sbuf("hT", [128, 16, S], BF16)
        self.norm_mod(res, lambda k, s: self.Amod[:, i, 1, k, s:s + 1], lambda k, s: self.mod_ap(i, 3, k, s), hT=hT)
        stg = [P.sbuf("stgb", [128, S], BF16) for _ in range(2)]
        tmpr = [P.sbuf("tmpr", [128, 512]) for _ in range(3)]
        cnt = [0]

        def evac(j, M, ti, t0, tl, ps, pk):
            st = stg[j % 2]
            n = cnt[0] % 3
            cnt[0] += 1
            tmp = tmpr[n]
            P.op("act", lambda e, tmp=tmp, ps=ps, tl=tl: e.activation(out=tmp[:, 0:tl], in_=ps[:, 0:tl], func=AF.Relu),
                 reads=[pk], writes=[f"tmpr{n}"])
            P.op("dve", lambda e, st=st, tmp=tmp, t0=t0, tl=tl: e.tensor_tensor(
                out=st[:, t0:t0 + tl], in0=tmp[:, 0:tl], in1=tmp[:, 0:tl], op=ALU.mult),
                reads=[f"tmpr{n}"], writes=[(f"stgb{j % 2}", ti)])

        def row_done(j, M):
            P.dma("sp", aT[j * 128:(j + 1) * 128, :], stg[j % 2][:], reads=[(f"stgb{j % 2}", ti) for ti in range(9)])

        self.linear_hT(w_up, DFF, hT, evac, row_done)
        P.pop()
        P.push()
        ab = P.sbuf("ab", [128, 64, 1024], BF16)
        wdb = [P.sbuf("wd", [128, 64, 128], BF16) for _ in range(2)]
        rp = [P.sbuf("rp", [128, 512]) for _ in range(4)]
        av = aT.rearrange("(k p) t -> p k t", p=128)
        wv = w_down.rearrange("(k p) d -> p k d", p=128)
        blocks = [(0, 256)] + [(256 + 1024 * k, 1024) for k in range(4)]
        nw = 0
        nr = 0
        for (b0, bl) in blocks:
            s = 1 if b0 < NCTX else 0
            for k0 in range(0, 64, 16):
                P.dma("sp", ab[:, k0:k0 + 16, 0:bl], av[:, k0:k0 + 16, b0:b0 + bl],
                      writes=[("ab", k0)])
            for d in range(16):
                wt, wk = wdb[nw % 2], f"wd{nw % 2}"
                nw += 1
                P.dma("pool", wt[:], wv[:, :, d * 128:(d + 1) * 128], writes=[wk])
                for c0 in range(0, bl, 512):
                    cl = min(512, bl - c0)
                    r, rk = rp[nr % 4], f"rp{nr % 4}"
                    nr += 1
                    P.dma("sp", r[:, 0:cl], res[d * 128:(d + 1) * 128, b0 + c0:b0 + c0 + cl], writes=[rk])
                    ps, pk = self.next_ps()
                    for k in range(64):
                        P.op("pe", lambda e, ps=ps, wt=wt, k=k, c0=c0, cl=cl: e.matmul(
                            ps[:, 0:cl], lhsT=wt[:, k, :], rhs=ab[:, k, c0:c0 + cl], start=(k == 0), stop=(k == 63)),
                            reads=[wk, ("ab", (k // 16) * 16)], writes=[pk])
                    P.op("dve", lambda e, ps=ps, r=r, cl=cl, d=d, s=s: e.scalar_tensor_tensor(
                        out=r[:, 0:cl], in0=ps[:, 0:cl], scalar=self.mod_ap(i, 5, d, s), in1=r[:, 0:cl],
                        op0=ALU.mult, op1=ALU.add), reads=[pk, rk, "modT"], writes=[rk])
                    P.dma("sp", res[d * 128:(d + 1) * 128, b0 + c0:b0 + c0 + cl], r[:, 0:cl], reads=[rk])
        P.pop()

    def lru_mixer(self, pT, yT, w_gate):
        P = self.P
        P.push()
        xr = [P.sbuf("xr", [128, S]) for _ in range(2)]
        xrb = [P.sbuf("xrb", [128, S], BF16) for _ in range(2)]
        wg = P.sbuf("wg", [128, 2, 4, 256], BF16)
        T = [P.sbuf("lt", [128, S]) for _ in range(5)]
        ybf = P.sbuf("ybf", [128, S], BF16)
        ncl = P.sbuf("ncl", [128, 32])
        P.op("act", lambda e: e.activation(out=ncl[:], in_=self.cs("llam", 0, 32), func=AF.Exp, scale=-1.0),
             reads=["consts"], writes=["ncl"])
        P.op("act", lambda e: e.activation(out=ncl[:], in_=ncl[:], func=AF.Ln, scale=1.0, bias=self.one_col[:]),
             reads=["ncl", "one_col"], writes=["ncl"])
        P.op("dve", lambda e: e.tensor_scalar(out=ncl[:], in0=ncl[:], scalar1=-8.0, scalar2=None, op0=ALU.mult),
             reads=["ncl"], writes=["ncl"])
        segs = [(0, NCTX), (NCTX, S)]
        for n in range(8):
            for dg in range(4):
                P.dma("pool", wg[:, :, dg, :], w_gate[dg // 2, dg % 2, n].rearrange("(kc p) j -> p kc j", p=128),
                      writes=["wg"])
            for kc in range(2):
                c = 2 * n + kc
                px, pxk = T[4], "lt4"
                P.dma("sp", px[:], pT[D + c * 128:D + (c + 1) * 128, :], writes=[pxk])
                x_, xk = xr[kc], f"xr{kc}"
                P.op("dve", lambda e, x_=x_, px=px, c=c: e.tensor_scalar(
                    out=x_[:], in0=px[:], scalar1=self.cs("lconv", 2 * 16 + c), scalar2=self.cs("lconvb", c),
                    op0=ALU.mult, op1=ALU.add), reads=[pxk, "consts"], writes=[xk])
                for (a, b_) in segs:
                    for tap, sh in ((0, -2), (1, -1), (3, 1)):
                        lo = max(a, a - sh)
                        hi = min(b_, b_ - sh)
                        P.op("dve", lambda e, x_=x_, px=px, c=c, tap=tap, sh=sh, lo=lo, hi=hi: e.scalar_tensor_tensor(
                            out=x_[:, lo:hi], in0=px[:, lo + sh:hi + sh], scalar=self.cs("lconv", tap * 16 + c),
                            in1=x_[:, lo:hi], op0=ALU.mult, op1=ALU.add), reads=[pxk, xk, "consts"], writes=[xk])
                P.op("pool", lambda e, kc=kc, x_=x_: e.tensor_copy(out=xrb[kc][:], in_=x_[:]), reads=[xk], writes=[f"xrb{kc}"])
            for jc in range(2):
                cj = 2 * n + jc
                for dr in range(2):
                    for gate in range(2):
                        Tg, tgk = (T[0], "lt0") if gate == 0 else (T[1], "lt1")
                        for ti, (t0, tl) in enumerate(TOK_TILES):
                            ps, pk = self.next_ps()
                            for kc in range(2):
                                P.op("pe", lambda e, ps=ps, kc=kc, dr=dr, gate=gate, jc=jc, t0=t0, tl=tl: e.matmul(
                                    ps[:, 0:tl], lhsT=wg[:, kc, dr * 2 + gate, jc * 128:(jc + 1) * 128],
                                    rhs=xrb[kc][:, t0:t0 + tl], start=(kc == 0), stop=(kc == 1)),
                                    reads=["wg", f"xrb{kc}"], writes=[pk])
                            P.op("act", lambda e, Tg=Tg, ps=ps, t0=t0, tl=tl, dr=dr, gate=gate, cj=cj: e.activation(
                                out=Tg[:, t0:t0 + tl], in_=ps[:, 0:tl], func=AF.Sigmoid,
                                bias=self.cs("lbg", (dr * 2 + gate) * 16 + cj), scale=1.0),
                                reads=[pk, "consts"], writes=[tgk])
                    P.op("act", lambda e, dr=dr, cj=cj: e.activation(out=T[0][:], in_=T[0][:], func=AF.Exp,
                                                                     scale=ncl[:, dr * 16 + cj:dr * 16 + cj + 1]),
                         reads=["lt0", "ncl"], writes=["lt0"])
                    P.op("dve", lambda e: e.tensor_tensor(out=T[2][:], in0=T[0][:], in1=T[0][:], op=ALU.mult),
                         reads=["lt0"], writes=["lt2"])
                    P.op("dve", lambda e: e.tensor_scalar(out=T[2][:], in0=T[2][:], scalar1=-1.0, scalar2=1.0,
                                                          op0=ALU.mult, op1=ALU.add), reads=["lt2"], writes=["lt2"])
                    P.op("dve", lambda e: e.tensor_scalar(out=T[2][:], in0=T[2][:], scalar1=0.0, scalar2=None, op0=ALU.max),
                         reads=["lt2"], writes=["lt2"])
                    P.op("act", lambda e: e.activation(out=T[2][:], in_=T[2][:], func=AF.Sqrt), reads=["lt2"], writes=["lt2"])
                    P.op("dve", lambda e: e.tensor_tensor(out=T[1][:], in0=T[1][:], in1=T[2][:], op=ALU.mult),
                         reads=["lt1", "lt2"], writes=["lt1"])
                    P.op("dve", lambda e, jc=jc: e.tensor_tensor(out=T[1][:], in0=T[1][:], in1=xr[jc][:], op=ALU.mult),
                         reads=["lt1", f"xr{jc}"], writes=["lt1"])
                    if dr == 0:
                        P.op("dve", lambda e: e.tensor_tensor_scan(out=T[3][:], data0=T[0][:], data1=T[1][:], initial=0.0,
                                                                   op0=ALU.mult, op1=ALU.add),
                             reads=["lt0", "lt1"], writes=["lt3"])
                    else:
                        P.op("dve", lambda e: e.tensor_tensor_scan(
                            out=T[2][:, NCTX - 1::-1], data0=T[0][:, NCTX - 1::-1], data1=T[1][:, NCTX - 1::-1],
                            initial=0.0, op0=ALU.mult, op1=ALU.add), reads=["lt0", "lt1"], writes=["lt2"])
                        P.op("dve", lambda e: e.tensor_tensor_scan(
                            out=T[2][:, S - 1:NCTX - 1:-1], data0=T[0][:, S - 1:NCTX - 1:-1], data1=T[1][:, S - 1:NCTX - 1:-1],
                            initial=T[2][:, 0:1], op0=ALU.mult, op1=ALU.add), reads=["lt0", "lt1", "lt2"], writes=["lt2"])
                        P.op("dve", lambda e: e.tensor_tensor(out=T[3][:], in0=T[3][:], in1=T[2][:], op=ALU.add),
                             reads=["lt3", "lt2"], writes=["lt3"])
                py, pyk = T[4], "lt4"
                P.dma("sp", py[:], pT[cj * 128:(cj + 1) * 128, :], writes=[pyk])
                P.op("act", lambda e: e.activation(out=T[2][:], in_=T[4][:], func=AF.Square), reads=["lt4"], writes=["lt2"])
                P.op("dve", lambda e: e.tensor_scalar(out=T[2][:], in0=T[2][:], scalar1=0.044715, scalar2=1.0,
                                                      op0=ALU.mult, op1=ALU.add), reads=["lt2"], writes=["lt2"])
                P.op("dve", lambda e: e.tensor_tensor(out=T[2][:], in0=T[2][:], in1=T[4][:], op=ALU.mult),
                     reads=["lt2", "lt4"], writes=["lt2"])
                P.op("act", lambda e: e.activation(out=T[2][:], in_=T[2][:], func=AF.Sigmoid, scale=1.5957691216057308),
                     reads=["lt2"], writes=["lt2"])
                P.op("dve", lambda e: e.tensor_tensor(out=T[2][:], in0=T[2][:], in1=T[4][:], op=ALU.mult),
                     reads=["lt2", "lt4"], writes=["lt2"])
                P.op("dve", lambda e: e.tensor_tensor(out=ybf[:], in0=T[2][:], in1=T[3][:], op=ALU.mult),
                     reads=["lt2", "lt3"], writes=["ybf"])
                P.dma("sp", yT[cj * 128:(cj + 1) * 128, :], ybf[:], reads=["ybf"])
        P.pop()


def EPS_AP(self):
    return self.eps_col[:]


def build_mixer_test(kind, col=False, lj=0):
    B = Builder()
    P = B.P
    B.setup()
    hin = B.ext_in("hin", [D, S])
    yout = B.ext_out("yout", [D, S])
    if kind == "gdn":
        w_in = B.ext_in("w_in", [D, GDN_IN])
        w_out = B.ext_in("w_out", [2 * D, D])
        F, VW = GDN_IN, 2 * D
    if kind == "ml":
        w_in = B.ext_in("w_in", [D, ML_IN])
        w_out = B.ext_in("w_out", [D, D])
        F, VW = ML_IN, D
    if kind == "lru":
        w_in = B.ext_in("w_in", [D, LRU_IN])
        w_gate = B.ext_in("w_gate", [2, 2, 8, 256, 256])
        w_out = B.ext_in("w_out", [D, D])
        F, VW = LRU_IN, D
    pT = P.dram("pT", [F, S])
    yT = P.dram("yT", [VW, S], BF16)
    P.push()
    hT = P.sbuf("hT", [128, 16, S], BF16)
    P.dma("pool", hT[:], hin.rearrange("(k p) t -> p k t", p=128), writes=["hT_all"])
    P.barrier()
    B.in_proj(w_in, F, hT, pT, col)
    P.pop()
    if kind == "lru":
        B.lru_mixer(pT, yT, w_gate)
    if kind == "ml":
        B.mlstm_mixer(pT, yT, col)
    import os as _os
    if kind == "gdn" and int(_os.environ.get("GDN_ST", "9")) > 0:
        B.gdn_mixer(lj, pT, yT, col)
    B.out_proj(yT, w_out, VW, None, None, dump=yout)
    return P.finish()


def conv_silu_row(self, dst, dkey, px, pxk, src_rows, wcol_fn, silu=True):
    P = self.P
    P.dma("sp", px, src_rows, writes=[pxk])
    P.op("dve", lambda e: e.tensor_scalar(out=dst[:], in0=px, scalar1=wcol_fn(2), scalar2=None, op0=ALU.mult),
         reads=[pxk, "consts"], writes=[dkey])
    for (a, b_) in ((0, NCTX), (NCTX, S)):
        for tap, sh in ((0, -2), (1, -1), (3, 1)):
            lo = max(a, a - sh)
            hi = min(b_, b_ - sh)
            P.op("dve", lambda e, tap=tap, sh=sh, lo=lo, hi=hi: e.scalar_tensor_tensor(
                out=dst[:, lo:hi], in0=px[:, lo + sh:hi + sh], scalar=wcol_fn(tap), in1=dst[:, lo:hi],
                op0=ALU.mult, op1=ALU.add), reads=[pxk, dkey, "consts"], writes=[dkey])
    if silu:
        P.op("act", lambda e: e.activation(out=dst[:], in_=dst[:], func=AF.Silu), reads=[dkey], writes=[dkey])


def gdn_mixer(self, lj, pT, yT, col):
    import os as _os2
    P = self.P
    P.push()
    NCH = 34
    ident = self.cs("ident", 0, 128)

    def mask_ap(dr, strict):
        return self.cs("mask", (dr * 2 + strict) * 128, 128)

    onesf = P.sbuf("onesf", [128, 128])
    P.op("dve", lambda e: e.memset(onesf[:], 1.0), writes=["onesf"])
    GG = P.sbuf("GG", [64, S])
    negGT = P.sbuf("negGT", [128, NCH, 64])
    GT = P.sbuf("GT", [128, NCH, 64])
    eGLBT = P.sbuf("eGLBT", [128, NCH, 64])
    betaT = P.sbuf("betaT", [128, NCH, 64])
    negident = P.sbuf("negident", [128, 128])
    P.op("dve", lambda e: e.tensor_scalar(out=negident[:], in0=ident, scalar1=-1.0, scalar2=None, op0=ALU.mult),
         reads=["consts"], writes=["negident"])
    P.push()
    LB = P.sbuf("LB", [64, S])
    DL = P.sbuf("DL", [64, S])
    nea = P.sbuf("nea", [64, 1])
    tri = [P.sbuf("tri", [128, 128]) for _ in range(2)]
    dltb = [P.sbuf("dlt", [128, 128]) for _ in range(2)]
    glbb = [P.sbuf("glb", [128, 64]) for _ in range(2)]
    for dr in range(2):
        P.op("dve", lambda e, dr=dr: e.tensor_scalar(out=tri[dr][:], in0=mask_ap(dr, 0), scalar1=1.0 / 30000.0, scalar2=1.0,
                                                     op0=ALU.mult, op1=ALU.add), reads=["consts"], writes=[f"tri{dr}"])
    P.dma("sp", LB[:], pT[12288:12352, :], writes=["LB"])
    P.dma("sp", DL[:], pT[12352:12416, :], writes=["DL"])
    P.op("act", lambda e: e.activation(out=LB[:], in_=LB[:], func=AF.Exp, scale=-1.0), reads=["LB"], writes=["LB"])
    P.op("act", lambda e: e.activation(out=LB[:], in_=LB[:], func=AF.Ln, bias=self.one_col[0:64, :], scale=1.0),
         reads=["LB", "one_col"], writes=["LB"])
    P.op("dve", lambda e: e.tensor_scalar(out=LB[:], in0=LB[:], scalar1=-1.0, scalar2=None, op0=ALU.mult),
         reads=["LB"], writes=["LB"])
    P.op("act", lambda e: e.activation(out=DL[:], in_=DL[:], func=AF.Exp, bias=self.cs("gdt", lj)[0:64, :], scale=1.0),
         reads=["DL", "consts"], writes=["DL"])
    P.op("act", lambda e: e.activation(out=DL[:], in_=DL[:], func=AF.Ln, bias=self.one_col[0:64, :], scale=1.0),
         reads=["DL", "one_col"], writes=["DL"])
    P.op("act", lambda e: e.activation(out=nea[:], in_=self.cs("galog", lj)[0:64, :], func=AF.Exp), reads=["consts"], writes=["nea"])
    P.op("dve", lambda e: e.tensor_scalar(out=nea[:], in0=nea[:], scalar1=-1.0, scalar2=None, op0=ALU.mult), reads=["nea"], writes=["nea"])
    P.op("dve", lambda e: e.tensor_scalar(out=DL[:], in0=DL[:], scalar1=nea[:, 0:1], scalar2=None, op0=ALU.mult),
         reads=["DL", "nea"], writes=["DL"])
    for c in range(NCH):
        cs_ = slice(c * 128, (c + 1) * 128)
        dlt, kd_ = dltb[c % 2], f"dlt{c % 2}"
        glb, kg_ = glbb[c % 2], f"glb{c % 2}"
        ps, pk = self.next_ps()
        P.op("pe", lambda e, ps=ps, cs_=cs_: e.transpose(out=ps[:, 0:64], in_=DL[:, cs_], identity=ident[0:64, 0:64]),
             reads=["DL", "consts"], writes=[pk])
        P.op("pe", lambda e, ps=ps, cs_=cs_: e.transpose(out=ps[:, 64:128], in_=LB[:, cs_], identity=ident[0:64, 0:64]),
             reads=["LB", "consts"], writes=[pk])
        P.op("act", lambda e, ps=ps, dlt=dlt: e.activation(out=dlt[:], in_=ps[:, 0:128], func=AF.Copy), reads=[pk], writes=[kd_])
        ps2, pk2 = self.next_ps()
        P.op("pe", lambda e, ps2=ps2, dlt=dlt: e.matmul(ps2[:, 0:32], lhsT=tri[0][:], rhs=dlt[:, 0:32], start=True, stop=True),
             reads=["tri0", kd_], writes=[pk2])
        P.op("pe", lambda e, ps2=ps2, dlt=dlt: e.matmul(ps2[:, 32:64], lhsT=tri[1][:], rhs=dlt[:, 32:64], start=True, stop=True),
             reads=["tri1", kd_], writes=[pk2])
        P.op("act", lambda e, ps2=ps2, c=c: e.activation(out=GT[:, c, :], in_=ps2[:, 0:64], func=AF.Copy), reads=[pk2], writes=[("GT", c)])
        P.op("dve", lambda e, ps2=ps2, c=c: e.tensor_scalar(out=negGT[:, c, :], in0=ps2[:, 0:64], scalar1=-1.0, scalar2=None, op0=ALU.mult),
             reads=[pk2], writes=["negGT"])
        P.op("dve", lambda e, ps2=ps2, dlt=dlt, glb=glb: e.tensor_tensor(out=glb[:], in0=ps2[:, 0:64], in1=dlt[:, 64:128], op=ALU.add),
             reads=[pk2, kd_], writes=[kg_])
        P.op("act", lambda e, glb=glb, c=c: e.activation(out=eGLBT[:, c, :], in_=glb[:], func=AF.Exp), reads=[kg_], writes=["eGLBT"])
        P.op("act", lambda e, dlt=dlt, c=c: e.activation(out=betaT[:, c, :], in_=dlt[:, 64:128], func=AF.Exp), reads=[kd_], writes=["betaT"])
        ps3, pk3 = self.next_ps()
        P.op("pe", lambda e, ps3=ps3, c=c: e.transpose(out=ps3[0:64, 0:128], in_=GT[:, c, :], identity=ident), reads=[("GT", c), "consts"], writes=[pk3])
        P.op("dve", lambda e, ps3=ps3, cs_=cs_: e.tensor_copy(out=GG[:, cs_], in_=ps3[0:64, 0:128]), reads=[pk3], writes=["GG"])
    P.pop()
    QT = P.sbuf("QT", [128, S])
    KT = P.sbuf("KT", [128, S])
    VT = P.sbuf("VT", [128, S])
    SZ = P.sbuf("SZ", [128, S], BF16)
    HC = 9
    Ust = P.sbuf("Ust", [128, HC, 128])
    Wst = P.sbuf("Wst", [128, HC, 128])
    Ast = P.sbuf("Ast", [128, HC, 128])
    Kst = P.sbuf("Kst", [128, HC, 128])
    Qst = P.sbuf("Qst", [128, HC, 128])
    EGL = P.sbuf("EGL", [128, HC])
    Oacc = P.sbuf("Oacc", [128, NCH, 128])
    Oflat = Oacc[:].rearrange("p c e -> p (c e)")
    yrow = P.sbuf("yrow", [128, S], BF16)
    Sst = [P.sbuf("Sst", [128, 128]) for _ in range(2)]
    NT = 24
    tmps = [P.sbuf("tm", [128, 128]) for _ in range(NT)]
    tcnt = [0]

    def tmp():
        i = tcnt[0] % NT
        tcnt[0] += 1
        return tmps[i], f"tm{i}"

    tgs = [P.sbuf("tg", [64, 128]) for _ in range(2)]
    tgi = [0]
    roles = {nm: [P.sbuf("rl_" + nm, [128, 128]) for _ in range(2)] for nm in ("Nt", "N_", "X", "Vb")}

    def role(nm):
        i = tgi[0] % 2
        return roles[nm][i], f"rl_{nm}{i}"

    cols_ = [P.sbuf("cl", [128, 1]) for _ in range(8)]
    ccnt = [0]

    def colt():
        i = ccnt[0] % 8
        ccnt[0] += 1
        return cols_[i], f"cl{i}"

    ss = P.sbuf("ss", [128, NCH])
    dbgt = P.sbuf("dbgt", [128, 128])
    rsq = [P.sbuf("rsq", [128, 512]) for _ in range(2)]

    def l2norm_row(X, xk, scale):
        for ti, (t0, tl) in enumerate(TOK_TILES):
            sq, sk = rsq[ti % 2], f"rsq{ti % 2}"
            P.op("act", lambda e, sq=sq, t0=t0, tl=tl: e.activation(out=sq[:, 0:tl], in_=X[:, t0:t0 + tl], func=AF.Square),
                 reads=[xk], writes=[sk])
            ps, pk = self.next_ps()
            P.op("pe", lambda e, ps=ps, sq=sq, tl=tl: e.matmul(ps[:, 0:tl], lhsT=onesf[:], rhs=sq[:, 0:tl], start=True, stop=True),
                 reads=[sk, "onesf"], writes=[pk])
            P.op("act", lambda e, ps=ps, sq=sq, tl=tl: e.activation(out=sq[:, 0:tl], in_=ps[:, 0:tl], func=AF.Ln, bias=self.eps_col[:], scale=1.0),
                 reads=[pk, "eps"], writes=[sk])
            P.op("act", lambda e, sq=sq, tl=tl: e.activation(out=sq[:, 0:tl], in_=sq[:, 0:tl], func=AF.Exp, scale=-0.5),
                 reads=[sk], writes=[sk])
            P.op("dve", lambda e, sq=sq, t0=t0, tl=tl: e.scalar_tensor_tensor(
                out=X[:, t0:t0 + tl], in0=X[:, t0:t0 + tl], scalar=scale, in1=sq[:, 0:tl], op0=ALU.mult, op1=ALU.mult),
                reads=[xk, sk], writes=[xk])

    def gcol(chunk):
        return lambda tap: self.cs("gconv", (lj * 4 + tap) * 64 + chunk)

    order_f = list(range(NCH))
    order_b = [1, 0] + list(range(NCH - 1, 1, -1))

    DBG_U = int(_os2.environ.get("GDN_U", "100000"))
    ucnt = [0]

    def uop(*a_, **k_):
        ucnt[0] += 1
        if ucnt[0] > DBG_U:
            return None
        return P.op(*a_, **k_)

    def unit(vh, dr, sl, c):
        ucnt[0] = 0
        r = dr * 32 + vh
        cs_ = slice(c * 128, (c + 1) * 128)
        last = 127 if dr == 0 else 0
        tgi[0] += 1
        t_g, kg = tgs[tgi[0] % 2], f"tg{tgi[0] % 2}"
        uop("dve", lambda e: e.tensor_scalar(out=t_g[0:64, :], in0=GG[0:64, cs_], scalar1=ident[0:64, r:r + 1], scalar2=None, op0=ALU.mult),
             reads=["GG", "consts"], writes=[kg])
        psA, kA = self.next_ps()
        uop("pe", lambda e: e.matmul(psA[:, 0:128], lhsT=onesf[0:64, :], rhs=t_g[0:64, :], start=True, stop=False), reads=[kg, "onesf"], writes=[kA])
        uop("pe", lambda e: e.matmul(psA[:, 0:128], lhsT=ident, rhs=mask_ap(dr, 0), start=False, stop=True), reads=["consts"], writes=[kA])
        uop("pe", lambda e: e.matmul(psA[:, 128:256], lhsT=onesf[0:64, :], rhs=t_g[0:64, :], start=True, stop=False), reads=[kg, "onesf"], writes=[kA])
        uop("pe", lambda e: e.matmul(psA[:, 128:256], lhsT=negident[:], rhs=mask_ap(1 - dr, 1), start=False, stop=True), reads=["consts", "negident"], writes=[kA])
        uop("pe", lambda e: e.matmul(psA[:, 256:384], lhsT=onesf[0:64, :], rhs=t_g[0:64, :], start=True, stop=True), reads=[kg, "onesf"], writes=[kA])
        ngc = negGT[:, c, r:r + 1]
        Dm, kDm = tmp()
        uop("act", lambda e: e.activation(out=Dm[:], in_=psA[:, 0:128], func=AF.Exp, bias=ngc, scale=1.0), reads=[kA, "negGT"], writes=[kDm])
        DmB, kDmB = tmp()
        uop("act", lambda e: e.activation(out=DmB[:], in_=psA[:, 128:256], func=AF.Exp, bias=GT[:, c, r:r + 1], scale=-1.0), reads=[kA, ("GT", c)], writes=[kDmB])
        EG, kEG = tmp()
        uop("act", lambda e: e.activation(out=EG[:], in_=psA[:, 256:384], func=AF.Exp), reads=[kA], writes=[kEG])
        glc, kglc = colt()
        uop("act", lambda e: e.activation(out=glc[:], in_=psA[:, 256 + last:256 + last + 1], func=AF.Copy), reads=[kA], writes=[kglc])
        kdc, kkdc = colt()
        uop("act", lambda e: e.activation(out=kdc[:], in_=ngc, func=AF.Exp, bias=glc[:], scale=1.0), reads=[kglc, "negGT"], writes=[kkdc])
        uop("dve", lambda e: e.tensor_tensor(out=Qst[:, sl, :], in0=QT[:, cs_], in1=EG[:], op=ALU.mult), reads=["QT", kEG], writes=[("Q", sl)])
        uop("dve", lambda e: e.tensor_copy(out=EGL[:, sl:sl + 1], in_=EG[:, last:last + 1]), reads=[kEG], writes=[("EGL", sl)])
        psK, kK = self.next_ps()
        uop("pe", lambda e: e.matmul(psK[:, 0:128], lhsT=KT[:, cs_], rhs=KT[:, cs_], start=True, stop=True), reads=["KT"], writes=[kK])
        uop("pe", lambda e: e.matmul(psK[:, 128:256], lhsT=KT[:, cs_], rhs=QT[:, cs_], start=True, stop=True), reads=["KT", "QT"], writes=[kK])
        uop("dve", lambda e: e.tensor_tensor(out=Ast[:, sl, :], in0=psK[:, 128:256], in1=Dm[:], op=ALU.mult), reads=[kK, kDm], writes=[("A", sl)])
        Nt, kNt = role("Nt")
        uop("dve", lambda e: e.scalar_tensor_tensor(out=Nt[:], in0=psK[:, 0:128], scalar=betaT[:, c, r:r + 1], in1=DmB[:], op0=ALU.mult, op1=ALU.mult),
             reads=[kK, kDmB, "betaT"], writes=[kNt])
        psT, kT = self.next_ps()
        uop("pe", lambda e: e.transpose(out=psT[:, 0:128], in_=KT[:, cs_], identity=ident), reads=["KT", "consts"], writes=[kT])
        uop("pe", lambda e: e.transpose(out=psT[:, 128:256], in_=VT[:, cs_], identity=ident), reads=["VT", "consts"], writes=[kT])
        psT2, kT2 = self.next_ps()
        _v = _os2.environ.get("GDN_V", "")
        if _v == "C":
            uop("pe", lambda e: e.transpose(out=psT2[:, 0:128], in_=KT[:, cs_], identity=ident), reads=[kNt, "KT", "consts"], writes=[kT2])
        else:
            uop("pe", lambda e: e.transpose(out=psT2[:, 0:128], in_=Nt[:], identity=ident), reads=[kNt, "consts"], writes=[kT2])
        X, kX = role("X")
        uop("dve", lambda e: e.tensor_scalar(out=X[:], in0=psT[:, 0:128], scalar1=eGLBT[:, c, r:r + 1], scalar2=None, op0=ALU.mult),
             reads=[kT, "eGLBT"], writes=[kX])
        uop("dve", lambda e: e.tensor_scalar(out=Kst[:, sl, :], in0=psT[:, 0:128], scalar1=kdc[:, 0:1], scalar2=None, op0=ALU.mult),
             reads=[kT, kkdc], writes=[("K", sl)])
        Vb, kVb = role("Vb")
        uop("dve", lambda e: e.tensor_scalar(out=Vb[:], in0=psT[:, 128:256], scalar1=betaT[:, c, r:r + 1], scalar2=None, op0=ALU.mult),
             reads=[kT, "betaT"], writes=[kVb])
        N_, kN = role("N_")
        if _v == "B":
            uop("dve", lambda e: e.tensor_copy(out=dbgt[:], in_=psT2[:, 0:128]), reads=[kT2], writes=["dbgt"])
        else:
            uop("dve", lambda e: e.tensor_copy(out=N_[:], in_=psT2[:, 0:128]), reads=[kT2], writes=[kN])
        def bm(idx):
            return self.cs("bmask", idx * 128, 128)

        def mm_to(lhsT, klh, rhs, krh):
            ps_, pk_ = self.next_ps()
            uop("pe", lambda e, ps_=ps_, lhsT=lhsT, rhs=rhs: e.matmul(ps_[:, 0:128], lhsT=lhsT[:], rhs=rhs[:], start=True, stop=True),
                reads=[klh, krh], writes=[pk_])
            return ps_, pk_

        def evac(ps_, pk_, eng):
            t_, kt_ = tmp()
            if eng == "act":
                uop("act", lambda e, t_=t_, ps_=ps_: e.activation(out=t_[:], in_=ps_[:, 0:128], func=AF.Copy), reads=[pk_], writes=[kt_])
            else:
                uop("dve", lambda e, t_=t_, ps_=ps_: e.tensor_copy(out=t_[:], in_=ps_[:, 0:128]), reads=[pk_], writes=[kt_])
            return t_, kt_

        def comb(ps_, pk_, base, kbase, op):
            t_, kt_ = tmp()
            uop("dve", lambda e, t_=t_, ps_=ps_, base=base: e.tensor_tensor(out=t_[:], in0=base[:], in1=ps_[:, 0:128], op=op),
                reads=[pk_, kbase], writes=[kt_])
            return t_, kt_

        def masked(src, ksrc, midx):
            t_, kt_ = tmp()
            uop("dve", lambda e, t_=t_, src=src: e.tensor_tensor(out=t_[:], in0=src[:], in1=bm(midx), op=ALU.mult), reads=[ksrc, "consts"], writes=[kt_])
            return t_, kt_

        Md, kMd = masked(Nt, kNt, 0)
        Mdt, kMdt = masked(N_, kN, 0)
        ImMd, kImMd = tmp()
        uop("dve", lambda e: e.scalar_tensor_tensor(out=ImMd[:], in0=Md[:], scalar=-1.0, in1=ident, op0=ALU.mult, op1=ALU.add), reads=[kMd, "consts"], writes=[kImMd])
        ImMdt, kImMdt = tmp()
        uop("dve", lambda e: e.scalar_tensor_tensor(out=ImMdt[:], in0=Mdt[:], scalar=-1.0, in1=ident, op0=ALU.mult, op1=ALU.add), reads=[kMdt, "consts"], writes=[kImMdt])
        p_, pk_ = mm_to(Mdt, kMdt, Md, kMd)
        P1, kP1 = evac(p_, pk_, "act")
        p_, pk_ = mm_to(Md, kMd, Mdt, kMdt)
        P1t, kP1t = evac(p_, pk_, "dve")
        p_, pk_ = mm_to(ImMdt, kImMdt, P1, kP1)
        Tn, kTn = comb(p_, pk_, ImMd, kImMd, ALU.add)
        p_, pk_ = mm_to(ImMd, kImMd, P1t, kP1t)
        TT, kTT = comb(p_, pk_, ImMdt, kImMdt, ALU.add)
        p_, pk_ = mm_to(P1t, kP1t, P1, kP1)
        P2, kP2 = evac(p_, pk_, "act")
        p_, pk_ = mm_to(P1, kP1, P1t, kP1t)
        P2t, kP2t = evac(p_, pk_, "dve")
        p_, pk_ = mm_to(TT, kTT, P2, kP2)
        Tn2, kTn2 = comb(p_, pk_, Tn, kTn, ALU.add)
        p_, pk_ = mm_to(Tn, kTn, P2t, kP2t)
        TT2, kTT2 = comb(p_, pk_, TT, kTT, ALU.add)
        Tn, kTn, TT, kTT = Tn2, kTn2, TT2, kTT2
        for li in range(4):
            iB = 1 + li if dr == 0 else 5 + li
            iBt = 5 + li if dr == 0 else 1 + li
            Bm, kBm = masked(Nt, kNt, iB)
            Btm, kBtm = masked(N_, kN, iBt)
            p_, pk_ = mm_to(Btm, kBtm, Tn, kTn)
            W_, kW_ = evac(p_, pk_, "act")
            p_, pk_ = mm_to(Bm, kBm, TT, kTT)
            X_, kX_ = evac(p_, pk_, "dve")
            p_, pk_ = mm_to(TT, kTT, W_, kW_)
            Tn2, kTn2 = comb(p_, pk_, Tn, kTn, ALU.subtract)
            p_, pk_ = mm_to(Tn, kTn, X_, kX_)
            TT2, kTT2 = comb(p_, pk_, TT, kTT, ALU.subtract)
            Tn, kTn, TT, kTT = Tn2, kTn2, TT2, kTT2
        psF, kF = self.next_ps()
        uop("pe", lambda e: e.matmul(psF[:, 0:128], lhsT=TT[:], rhs=Vb[:], start=True, stop=True), reads=[kTT, kVb], writes=[kF])
        uop("pe", lambda e: e.matmul(psF[:, 128:256], lhsT=X[:], rhs=TT[:], start=True, stop=True), reads=[kTT, kX], writes=[kF])
        uop("act", lambda e: e.activation(out=Ust[:, sl, :], in_=psF[:, 0:128], func=AF.Copy), reads=[kF], writes=[("U", sl)])
        uop("dve", lambda e: e.tensor_copy(out=Wst[:, sl, :], in_=psF[:, 128:256]), reads=[kF], writes=[("W", sl)])

    def seq_step(dr, sl, c, Sc, kSc, Sn, kSn):
        psW, kW = self.next_ps()
        P.op("pe", lambda e: e.matmul(psW[:, 0:128], lhsT=Wst[:, sl, :], rhs=Sc[:], start=True, stop=True), reads=[("W", sl), kSc], writes=[kW])
        vn, kvn = tmp()
        P.op("dve", lambda e: e.tensor_tensor(out=vn[:], in0=Ust[:, sl, :], in1=psW[:, 0:128], op=ALU.subtract), reads=[("U", sl), kW], writes=[kvn])
        P.op("pe", lambda e: e.matmul(psW[:, 128:256], lhsT=Qst[:, sl, :], rhs=Sc[:], start=True, stop=False), reads=[("Q", sl), kSc], writes=[kW])
        P.op("pe", lambda e: e.matmul(psW[:, 128:256], lhsT=Ast[:, sl, :], rhs=vn[:], start=False, stop=True), reads=[("A", sl), kvn], writes=[kW])
        if dr == 0:
            P.op("act", lambda e: e.activation(out=Oacc[:, c, :], in_=psW[:, 128:256], func=AF.Copy), reads=[kW], writes=[("O", c)])
        else:
            P.op("dve", lambda e: e.tensor_tensor(out=Oacc[:, c, :], in0=Oacc[:, c, :], in1=psW[:, 128:256], op=ALU.add),
                 reads=[kW, ("O", c)], writes=[("O", c)])
        P.op("pe", lambda e: e.matmul(psW[:, 256:384], lhsT=Kst[:, sl, :], rhs=vn[:], start=True, stop=True), reads=[("K", sl), kvn], writes=[kW])
        P.op("dve", lambda e: e.scalar_tensor_tensor(out=Sn[:], in0=Sc[:], scalar=EGL[:, sl:sl + 1], in1=psW[:, 256:384], op0=ALU.mult, op1=ALU.add),
             reads=[kSc, ("EGL", sl), kW], writes=[kSn])

    okeys = [("O", c) for c in range(NCH)]
    P.op("dve", lambda e: e.memset(Oflat, 0.0), writes=okeys + ["Oflat"])
    import os as _os
    DBG_QH = int(_os.environ.get("GDN_QH", "16"))
    DBG_ST = int(_os.environ.get("GDN_ST", "9"))
    for qh in range(DBG_QH):
        P.op("dve", lambda e: e.tensor_copy(out=Oflat[:, 0:1], in_=Oflat[:, 0:1]), reads=okeys + ["Oflat"], writes=["Oflat"])
        conv_silu_row(self, QT, "QT", Oflat, "Oflat", pT[qh * 128:(qh + 1) * 128, :], gcol(qh))
        l2norm_row(QT, "QT", 128 ** -0.5)
        conv_silu_row(self, KT, "KT", Oflat, "Oflat", pT[D + qh * 128:D + (qh + 1) * 128, :], gcol(16 + qh))
        l2norm_row(KT, "KT", 1.0)
        for vh in (2 * qh, 2 * qh + 1):
            P.op("dve", lambda e: e.tensor_copy(out=Oflat[:, 0:1], in_=Oflat[:, 0:1]), reads=okeys + ["Oflat"], writes=["Oflat"])
            conv_silu_row(self, VT, "VT", Oflat, "Oflat", pT[2 * D + vh * 128:2 * D + (vh + 1) * 128, :], gcol(32 + vh))
            P.dma("sp", Oflat, pT[4 * D + vh * 128:4 * D + (vh + 1) * 128, :], reads=["Oflat"], writes=["Oflat"])
            P.op("act", lambda e: e.activation(out=SZ[:], in_=Oflat, func=AF.Silu), reads=["Oflat"], writes=["SZ"])
            P.op("dve", lambda e: e.tensor_copy(out=Oflat[:, 0:1], in_=Oflat[:, 0:1]), reads=["Oflat"], writes=okeys + ["Oflat"])
            for dr in range(2 if DBG_ST >= 3 else 0):
                order = order_f if dr == 0 else order_b
                order = order[:int(_os2.environ.get("GDN_NCH", "34"))]
                P.op("dve", lambda e: e.memset(Sst[0][:], 0.0), writes=["S0"])
                si = 0
                for g0 in range(0, len(order), HC):
                    grp = order[g0:g0 + HC]
                    for sl, c in enumerate(grp):
                        unit(vh, dr, sl, c)
                    for sl, c in enumerate(grp):
                        if DBG_ST >= 4:
                            seq_step(dr, sl, c, Sst[si % 2], f"S{si % 2}", Sst[(si + 1) % 2], f"S{(si + 1) % 2}")
                        si += 1
            for c in range(NCH if DBG_ST >= 5 else 0):
                junk, kj = tmp()
                P.op("act", lambda e, junk=junk, c=c: e.activation(out=junk[:], in_=Oacc[:, c, :], func=AF.Square),
                     reads=[("O", c)], writes=[kj])
                P.op("dve", lambda e, junk=junk, c=c: e.reduce_sum(out=ss[:, c:c + 1], in_=junk[:], axis=mybir.AxisListType.X),
                     reads=[kj], writes=["ss"])
            if DBG_ST >= 5:
                P.op("act", lambda e: e.activation(out=ss[:], in_=ss[:], func=AF.Ln, bias=self.eps_col[:], scale=1.0 / 128), reads=["ss", "eps"], writes=["ss"])
                P.op("act", lambda e: e.activation(out=ss[:], in_=ss[:], func=AF.Exp, scale=-0.5), reads=["ss"], writes=["ss"])
            for c in range(NCH if DBG_ST >= 5 else 0):
                On, kOn = tmp()
                P.op("dve", lambda e, On=On, c=c: e.tensor_scalar(out=On[:], in0=Oacc[:, c, :], scalar1=ss[:, c:c + 1], scalar2=None, op0=ALU.mult),
                     reads=[("O", c), "ss"], writes=[kOn])
                ps, pk = self.next_ps()
                P.op("pe", lambda e, ps=ps, On=On: e.transpose(out=ps[:, 0:128], in_=On[:], identity=ident), reads=[kOn, "consts"], writes=[pk])
                if col and c >= 2:
                    lc = c - 2
                    o_ap = yrow[:, NCTX:].rearrange("p (r w) -> p w r", w=64)[:, 2 * lc:2 * lc + 2, :]
                    i0 = ps[:, 0:128].rearrange("p (w r) -> p w r", r=64)
                    i1 = SZ[:, c * 128:(c + 1) * 128].rearrange("p (w r) -> p w r", r=64)
                else:
                    o_ap = yrow[:, c * 128:(c + 1) * 128]
                    i0 = ps[:, 0:128]
                    i1 = SZ[:, c * 128:(c + 1) * 128]
                P.op("dve", lambda e, o_ap=o_ap, i0=i0, i1=i1: e.scalar_tensor_tensor(
                    out=o_ap, in0=i0, scalar=self.cs("gnorm", lj), in1=i1, op0=ALU.mult, op1=ALU.mult),
                    reads=[pk, "SZ", "consts"], writes=["yrow"])
            if DBG_ST >= 5:
                P.dma("sp", yT[vh * 128:(vh + 1) * 128, :], yrow[:], reads=["yrow"])
    P.pop()


Builder.gdn_mixer = gdn_mixer


def build_full(stop=None):
    B = Builder()
    P = B.P
    B.setup()
    xT = B.ext_in("xT", [D, S])
    outT = B.ext_out("outT", [D, NLAT]) if stop is None else None
    resdump = B.ext_out("resdump", [D, S]) if stop is not None else None
    w_mod = B.ext_in("w_mod", [DEPTH, D, 6 * D])
    gdn_w_in = B.ext_in("gdn_w_in", [2, D, GDN_IN])
    gdn_w_out = B.ext_in("gdn_w_out", [2, 2 * D, D])
    ml_w_in = B.ext_in("ml_w_in", [1, D, ML_IN])
    ml_w_out = B.ext_in("ml_w_out", [1, D, D])
    lru_w_in = B.ext_in("lru_w_in", [1, D, LRU_IN])
    lru_w_gate = B.ext_in("lru_w_gate", [1, 2, 2, 8, 256, 256])
    lru_w_out = B.ext_in("lru_w_out", [1, D, D])
    ff_w_up = B.ext_in("ff_w_up", [DEPTH, D, DFF])
    ff_w_down = B.ext_in("ff_w_down", [DEPTH, DFF, D])
    res = P.dram("resT", [D, S])
    pT = P.dram("pT", [GDN_IN, S])
    yT = P.dram("yT", [2 * D, S], BF16)
    aT = P.dram("aT", [DFF, S], BF16)
    P.dma("sp", res, xT)
    P.barrier()
    B.modulation(w_mod)
    for i in range(DEPTH):
        kind, j = i % 3, i // 3
        col = (i % 2 == 1)
        P.push()
        hT = P.sbuf("hT", [128, 16, S], BF16)
        B.norm_mod(res, lambda k, s, i=i: B.Amod[:, i, 0, k, s:s + 1], lambda k, s, i=i: B.mod_ap(i, 0, k, s), hT=hT)
        G_fn = lambda d, s, i=i: B.mod_ap(i, 2, d, s)
        if kind == 0:
            B.in_proj(gdn_w_in[j], GDN_IN, hT, pT, col)
            P.pop()
            B.gdn_mixer(j, pT, yT, col)
            B.out_proj(yT, gdn_w_out[j], 2 * D, G_fn, res)
        elif kind == 1:
            B.in_proj(ml_w_in[j], ML_IN, hT, pT[0:ML_IN, :], col)
            P.pop()
            B.mlstm_mixer(pT[0:ML_IN, :], yT[0:D, :], col)
            B.out_proj(yT[0:D, :], ml_w_out[j], D, G_fn, res)
        else:
            B.in_proj(lru_w_in[j], LRU_IN, hT, pT[0:LRU_IN, :], col)
            P.pop()
            B.lru_mixer(pT[0:LRU_IN, :], yT[0:D, :], lru_w_gate[j])
            B.out_proj(yT[0:D, :], lru_w_out[j], D, G_fn, res)
        if stop == (i, "mix"):
            P.barrier()
            P.dma("sp", resdump, res)
            return P.finish()
        B.ffn(i, ff_w_up[i], ff_w_down[i], res, aT)
        if stop == (i, "ffn"):
            P.barrier()
            P.dma("sp", resdump, res)
            return P.finish()
    B.norm_mod(res, lambda k, s: B.cs("nout", k), lambda k, s: B.zero_col[:], dst=outT, dst_off=NCTX)
    return P.finish()


def kernel(**inputs):
    inp = {k: np.asarray(v) for k, v in inputs.items()}
    nc = build_full()
    in_maps = []
    NCORE = 4
    for core in range(NCORE):
        b = core % 4
        xt = np.ascontiguousarray(np.concatenate([inp["ctx"][b], inp["x"][b]], axis=0).T.astype(np.float32))
        m = {"consts": make_consts(inp, b), "xT": xt}
        for k in ("w_mod", "gdn_w_in", "gdn_w_out", "ml_w_in", "ml_w_out", "lru_w_in", "lru_w_gate", "lru_w_out",
                  "ff_w_up", "ff_w_down"):
            m[k] = np.ascontiguousarray(inp[k], dtype=np.float32)
        in_maps.append(m)
    res = run_bass_kernel_spmd(nc, in_maps, core_ids=list(range(NCORE)))
    out = np.stack([np.ascontiguousarray(res.results[b]["outT"].T) for b in range(4)], axis=0)
    return out.astype(np.float32)


def build_gdn_only(col=False, lj=0):
    B = Builder()
    P = B.P
    B.setup()
    pT = B.ext_in("pT", [GDN_IN, S])
    yT = B.ext_out("yT", [2 * D, S], BF16)
    P.barrier()
    B.gdn_mixer(lj, pT, yT, col)
    return P.finish()


def mlstm_mixer(self, pT, yT, col):
    P = self.P
    P.push()
    NCH = 34
    ident = self.cs("ident", 0, 128)

    def mask_ap(dr, strict):
        return self.cs("mask", (dr * 2 + strict) * 128, 128)

    onesf = P.sbuf("onesf", [128, 128])
    P.op("dve", lambda e: e.memset(onesf[:], 1.0), writes=["onesf"])
    GT = [P.sbuf("mGT", [128, NCH, 32]) for _ in range(2)]
    NB = [P.sbuf("mNB", [128, NCH, 32]) for _ in range(2)]
    P.push()
    GR = P.sbuf("GR", [32, S])
    LF = P.sbuf("LF", [32, S])
    b15 = P.sbuf("b15", [32, 1])
    tri = [P.sbuf("tri", [128, 128]) for _ in range(2)]
    gtb = [P.sbuf("gtb", [128, 64]) for _ in range(2)]
    for dr in range(2):
        P.op("dve", lambda e, dr=dr: e.tensor_scalar(out=tri[dr][:], in0=mask_ap(dr, 0), scalar1=1.0 / 30000.0, scalar2=1.0,
                                                     op0=ALU.mult, op1=ALU.add), reads=["consts"], writes=[f"tri{dr}"])
    P.dma("sp", GR[:], pT[6144:6176, :], writes=["GR"])
    P.op("dve", lambda e: e.tensor_scalar(out=b15[:], in0=self.cs("mgb", 0)[0:32, :], scalar1=1.0 / 15.0, scalar2=None, op0=ALU.mult),
         reads=["consts"], writes=["b15"])
    W1 = P.sbuf("W1", [32, S])
    W2 = P.sbuf("W2", [32, S])
    W3 = P.sbuf("W3", [32, S])

    def dv(fn, reads, writes):
        P.op("dve", fn, reads=reads, writes=writes)

    P.op("dve", lambda e: e.tensor_scalar(out=b15[:], in0=b15[:], scalar1=2.0, scalar2=None, op0=ALU.mult), reads=["b15"], writes=["b15"])
    P.op("act", lambda e: e.activation(out=W1[:], in_=GR[:], func=AF.Exp, bias=b15[:], scale=2.0 / 15.0), reads=["GR", "b15"], writes=["W1"])
    dv(lambda e: e.tensor_scalar(out=W2[:], in0=W1[:], scalar1=1.0, scalar2=None, op0=ALU.add), ["W1"], ["W2"])
    dv(lambda e: e.reciprocal(out=W2[:], in_=W2[:]), ["W2"], ["W2"])
    dv(lambda e: e.tensor_scalar(out=W1[:], in0=W1[:], scalar1=-1.0, scalar2=15.0, op0=ALU.add, op1=ALU.mult), ["W1"], ["W1"])
    dv(lambda e: e.tensor_tensor(out=GR[:], in0=W1[:], in1=W2[:], op=ALU.mult), ["W1", "W2"], ["GR"])
    dv(lambda e: e.tensor_scalar(out=W1[:], in0=GR[:], scalar1=-1.0, scalar2=None, op0=ALU.mult), ["GR"], ["W1"])
    dv(lambda e: e.tensor_tensor(out=W2[:], in0=W1[:], in1=GR[:], op=ALU.max), ["W1", "GR"], ["W2"])
    P.op("act", lambda e: e.activation(out=W2[:], in_=W2[:], func=AF.Exp, scale=-1.0), reads=["W2"], writes=["W2"])
    dv(lambda e: e.tensor_scalar(out=W3[:], in0=W2[:], scalar1=2.0, scalar2=None, op0=ALU.add), ["W2"], ["W3"])
    dv(lambda e: e.reciprocal(out=W3[:], in_=W3[:]), ["W3"], ["W3"])
    dv(lambda e: e.tensor_tensor(out=W2[:], in0=W2[:], in1=W3[:], op=ALU.mult), ["W2", "W3"], ["W2"])
    dv(lambda e: e.tensor_tensor(out=W3[:], in0=W2[:], in1=W2[:], op=ALU.mult), ["W2"], ["W3"])
    dv(lambda e: e.tensor_scalar(out=LF[:], in0=W3[:], scalar1=1.0 / 11.0, scalar2=1.0 / 9.0, op0=ALU.mult, op1=ALU.add), ["W3"], ["LF"])
    for cst in (1.0 / 7.0, 1.0 / 5.0, 1.0 / 3.0, 1.0):
        dv(lambda e: e.tensor_tensor(out=LF[:], in0=LF[:], in1=W3[:], op=ALU.mult), ["LF", "W3"], ["LF"])
        dv(lambda e, cst=cst: e.tensor_scalar(out=LF[:], in0=LF[:], scalar1=cst, scalar2=None, op0=ALU.add), ["LF"], ["LF"])
    dv(lambda e: e.tensor_tensor(out=LF[:], in0=LF[:], in1=W2[:], op=ALU.mult), ["LF", "W2"], ["LF"])
    dv(lambda e: e.tensor_scalar(out=W1[:], in0=W1[:], scalar1=0.0, scalar2=None, op0=ALU.max), ["W1"], ["W1"])
    dv(lambda e: e.scalar_tensor_tensor(out=LF[:], in0=LF[:], scalar=2.0, in1=W1[:], op0=ALU.mult, op1=ALU.add), ["LF", "W1"], ["LF"])
    dv(lambda e: e.tensor_scalar(out=LF[:], in0=LF[:], scalar1=-1.0, scalar2=None, op0=ALU.mult), ["LF"], ["LF"])
    for c in range(NCH):
        cs_ = slice(c * 128, (c + 1) * 128)
        gt_, kgt = gtb[c % 2], f"gtb{c % 2}"
        ps, pk = self.next_ps()
        P.op("pe", lambda e, ps=ps, cs_=cs_: e.transpose(out=ps[:, 0:32], in_=GR[:, cs_], identity=ident[0:32, 0:32]), reads=["GR", "consts"], writes=[pk])
        P.op("pe", lambda e, ps=ps, cs_=cs_: e.transpose(out=ps[:, 32:64], in_=LF[:, cs_], identity=ident[0:32, 0:32]), reads=["LF", "consts"], writes=[pk])
        P.op("act", lambda e, ps=ps, gt_=gt_: e.activation(out=gt_[:], in_=ps[:, 0:64], func=AF.Copy), reads=[pk], writes=[kgt])
        ps2, pk2 = self.next_ps()
        P.op("pe", lambda e, ps2=ps2, gt_=gt_: e.matmul(ps2[:, 0:32], lhsT=tri[0][:], rhs=gt_[:, 32:64], start=True, stop=True), reads=["tri0", kgt], writes=[pk2])
        P.op("pe", lambda e, ps2=ps2, gt_=gt_: e.matmul(ps2[:, 32:64], lhsT=tri[1][:], rhs=gt_[:, 32:64], start=True, stop=True), reads=["tri1", kgt], writes=[pk2])
        for dr in range(2):
            P.op("dve", lambda e, ps2=ps2, c=c, dr=dr: e.tensor_copy(out=GT[dr][:, c, :], in_=ps2[:, dr * 32:(dr + 1) * 32]), reads=[pk2], writes=[("mGT", dr)])
            P.op("dve", lambda e, gt_=gt_, c=c, dr=dr: e.tensor_tensor(out=NB[dr][:, c, 0:24], in0=gt_[:, 0:24], in1=GT[dr][:, c, 8:32], op=ALU.subtract),
                 reads=[kgt, ("mGT", dr)], writes=[("mNB", dr)])
    P.pop()
    QT = P.sbuf("QT", [128, S])
    KT = P.sbuf("KT", [128, S])
    Vt = P.sbuf("Vt", [128, NCH, 257])
    Hacc = P.sbuf("Hacc", [128, NCH, 256])
    Hflat = Hacc[:].rearrange("p c e -> p (c e)")
    SO = P.sbuf("SO", [128, 2, S], BF16)
    HC = 9
    Ast = P.sbuf("Ast", [128, HC, 128])
    Kst = P.sbuf("Kst", [128, HC, 128])
    Qst = P.sbuf("Qst", [128, HC, 128])
    EGL = P.sbuf("EGL", [128, HC])
    yrow = [P.sbuf("yrow", [128, S], BF16) for _ in range(2)]
    Cst = [P.sbuf("Cst", [128, 257]) for _ in range(2)]
    NT = 16
    tmps = [P.sbuf("tm", [128, 128]) for _ in range(NT)]
    tcnt = [0]

    def tmp():
        i = tcnt[0] % NT
        tcnt[0] += 1
        return tmps[i], f"tm{i}"

    cols_ = [P.sbuf("cl", [128, 1]) for _ in range(12)]
    ccnt = [0]

    def colt():
        i = ccnt[0] % 12
        ccnt[0] += 1
        return cols_[i], f"cl{i}"

    hn = [P.sbuf("hn", [128, 256]) for _ in range(2)]
    ss = P.sbuf("ss", [128, NCH])
    hkeys = [("H", c) for c in range(NCH)]
    P.op("dve", lambda e: e.memset(Hflat, 0.0), writes=hkeys + ["Hflat"])
    P.op("dve", lambda e: e.memset(Vt[:, :, 256:257], 1.0), writes=["Vt1"])
    order_f = list(range(NCH))
    order_b = [1, 0] + list(range(NCH - 1, 1, -1))

    def unit(hd, dr, sl, c):
        rf = dr * 16 + 8 + hd
        ri = dr * 16 + hd
        cs_ = slice(c * 128, (c + 1) * 128)
        last = 127 if dr == 0 else 0
        dg, kdg = tmp()
        P.op("dve", lambda e: e.tensor_scalar(out=dg[:], in0=ident, scalar1=GT[dr][:, c, rf:rf + 1], scalar2=None, op0=ALU.mult),
             reads=["consts", ("mGT", dr)], writes=[kdg])
        psA, kA = self.next_ps()
        P.op("pe", lambda e: e.matmul(psA[:, 0:128], lhsT=onesf[:], rhs=dg[:], start=True, stop=False), reads=[kdg, "onesf"], writes=[kA])
        P.op("pe", lambda e: e.matmul(psA[:, 0:128], lhsT=ident, rhs=mask_ap(dr, 0), start=False, stop=True), reads=["consts"], writes=[kA])
        P.op("pe", lambda e: e.matmul(psA[:, 128:256], lhsT=onesf[:], rhs=dg[:], start=True, stop=True), reads=[kdg, "onesf"], writes=[kA])
        nbc = NB[dr][:, c, ri:ri + 1]
        Dm, kDm = tmp()
        P.op("act", lambda e: e.activation(out=Dm[:], in_=psA[:, 0:128], func=AF.Exp, bias=nbc, scale=1.0), reads=[kA, ("mNB", dr)], writes=[kDm])
        EG, kEG = tmp()
        P.op("act", lambda e: e.activation(out=EG[:], in_=psA[:, 128:256], func=AF.Exp), reads=[kA], writes=[kEG])
        glc, kglc = colt()
        P.op("act", lambda e: e.activation(out=glc[:], in_=psA[:, 128 + last:128 + last + 1], func=AF.Copy), reads=[kA], writes=[kglc])
        kdc, kkdc = colt()
        P.op("act", lambda e: e.activation(out=kdc[:], in_=nbc, func=AF.Exp, bias=glc[:], scale=1.0), reads=[kglc, ("mNB", dr)], writes=[kkdc])
        P.op("dve", lambda e: e.tensor_tensor(out=Qst[:, sl, :], in0=QT[:, cs_], in1=EG[:], op=ALU.mult), reads=["QT", kEG], writes=[("Q", sl)])
        P.op("dve", lambda e: e.tensor_copy(out=EGL[:, sl:sl + 1], in_=EG[:, last:last + 1]), reads=[kEG], writes=[("EGL", sl)])
        psK, kK = self.next_ps()
        P.op("pe", lambda e: e.matmul(psK[:, 0:128], lhsT=KT[:, cs_], rhs=QT[:, cs_], start=True, stop=True), reads=["KT", "QT"], writes=[kK])
        P.op("pe", lambda e: e.transpose(out=psK[:, 128:256], in_=KT[:, cs_], identity=ident), reads=["KT", "consts"], writes=[kK])
        P.op("dve", lambda e: e.tensor_tensor(out=Ast[:, sl, :], in0=psK[:, 0:128], in1=Dm[:], op=ALU.mult), reads=[kK, kDm], writes=[("A", sl)])
        P.op("dve", lambda e: e.tensor_scalar(out=Kst[:, sl, :], in0=psK[:, 128:256], scalar1=kdc[:, 0:1], scalar2=None, op0=ALU.mult),
             reads=[kK, kkdc], writes=[("K", sl)])

    def seq_step(dr, sl, c, Cc, kCc, Cn, kCn):
        psO, kO = self.next_ps()
        P.op("pe", lambda e: e.matmul(psO[:, 0:257], lhsT=Qst[:, sl, :], rhs=Cc[:], start=True, stop=False), reads=[("Q", sl), kCc], writes=[kO])
        P.op("pe", lambda e: e.matmul(psO[:, 0:257], lhsT=Ast[:, sl, :], rhs=Vt[:, c, :], start=False, stop=True), reads=[("A", sl), "Vt", "Vt1"], writes=[kO])
        dd, kdd = colt()
        P.op("dve", lambda e: e.tensor_scalar(out=dd[:], in0=psO[:, 256:257], scalar1=-1.0, scalar2=1.0, op0=ALU.mult, op1=ALU.max), reads=[kO], writes=[kdd])
        P.op("dve", lambda e: e.tensor_tensor(out=dd[:], in0=dd[:], in1=psO[:, 256:257], op=ALU.max), reads=[kO, kdd], writes=[kdd])
        P.op("dve", lambda e: e.reciprocal(out=dd[:], in_=dd[:]), reads=[kdd], writes=[kdd])
        P.op("dve", lambda e: e.scalar_tensor_tensor(out=Hacc[:, c, :], in0=psO[:, 0:256], scalar=dd[:, 0:1], in1=Hacc[:, c, :], op0=ALU.mult, op1=ALU.add),
             reads=[kO, kdd, ("H", c)], writes=[("H", c)])
        psS, kS = self.next_ps()
        P.op("pe", lambda e: e.matmul(psS[:, 0:257], lhsT=Kst[:, sl, :], rhs=Vt[:, c, :], start=True, stop=True), reads=[("K", sl), "Vt", "Vt1"], writes=[kS])
        P.op("dve", lambda e: e.scalar_tensor_tensor(out=Cn[:], in0=Cc[:], scalar=EGL[:, sl:sl + 1], in1=psS[:, 0:257], op0=ALU.mult, op1=ALU.add),
             reads=[kCc, ("EGL", sl), kS], writes=[kCn])

    import os as _os3
    for hd in range(int(_os3.environ.get("ML_H", "8"))):
        P.dma("sp", QT[:], pT[hd * 128:(hd + 1) * 128, :], writes=["QT"])
        P.op("dve", lambda e: e.tensor_scalar(out=QT[:], in0=QT[:], scalar1=128 ** -0.5, scalar2=None, op0=ALU.mult), reads=["QT"], writes=["QT"])
        P.dma("sp", KT[:], pT[1024 + hd * 128:1024 + (hd + 1) * 128, :], writes=["KT"])
        P.op("dve", lambda e: e.tensor_copy(out=Hflat[:, 0:1], in_=Hflat[:, 0:1]), reads=hkeys + ["Hflat"], writes=["Hflat"])
        P.dma("sp", Hflat[:, 0:S], pT[2048 + hd * 256:2048 + hd * 256 + 128, :], reads=["Hflat"], writes=["Hflat"])
        P.dma("sp", Hflat[:, S:2 * S], pT[2048 + hd * 256 + 128:2048 + (hd + 1) * 256, :], reads=["Hflat"], writes=["Hflat"])
        for c in range(NCH):
            ps, pk = self.next_ps()
            for hf in range(2):
                P.op("pe", lambda e, ps=ps, c=c, hf=hf: e.transpose(out=ps[:, hf * 128:(hf + 1) * 128], in_=Hflat[:, hf * S + c * 128:hf * S + (c + 1) * 128], identity=ident),
                     reads=["Hflat", "consts"], writes=[pk])
            self.evac_copy(self.alt_eng(), Vt[:, c, 0:256], ps[:, 0:256], [pk], ["Vt"])
        for hf in range(2):
            P.dma("sp", Hflat[:, 0:S], pT[4096 + hd * 256 + hf * 128:4096 + hd * 256 + (hf + 1) * 128, :], reads=["Hflat", "Vt"], writes=["Hflat"])
            P.op("act", lambda e, hf=hf: e.activation(out=SO[:, hf, :], in_=Hflat[:, 0:S], func=AF.Sigmoid), reads=["Hflat"], writes=["SO"])
        P.op("dve", lambda e: e.memset(Hflat, 0.0), reads=["Hflat"], writes=hkeys + ["Hflat"])
        for dr in range(2):
            order = order_f if dr == 0 else order_b
            P.op("dve", lambda e: e.memset(Cst[0][:], 0.0), writes=["C0"])
            si = 0
            for g0 in range(0, NCH, HC):
                grp = order[g0:g0 + HC]
                for sl, c in enumerate(grp):
                    unit(hd, dr, sl, c)
                for sl, c in enumerate(grp):
                    seq_step(dr, sl, c, Cst[si % 2], f"C{si % 2}", Cst[(si + 1) % 2], f"C{(si + 1) % 2}")
                    si += 1
        for c in range(NCH):
            h_, kh = hn[c % 2], f"hn{c % 2}"
            P.op("act", lambda e, h_=h_, c=c: e.activation(out=h_[:], in_=Hacc[:, c, :], func=AF.Square), reads=[("H", c)], writes=[kh])
            P.op("dve", lambda e, h_=h_, c=c: e.reduce_sum(out=ss[:, c:c + 1], in_=h_[:], axis=mybir.AxisListType.X), reads=[kh], writes=["ss"])
        P.op("act", lambda e: e.activation(out=ss[:], in_=ss[:], func=AF.Ln, bias=self.eps_col[:], scale=1.0 / 256), reads=["ss", "eps"], writes=["ss"])
        P.op("act", lambda e: e.activation(out=ss[:], in_=ss[:], func=AF.Exp, scale=-0.5), reads=["ss"], writes=["ss"])
        for c in range(NCH):
            h_, kh = hn[c % 2], f"hn{c % 2}"
            P.op("dve", lambda e, h_=h_, c=c: e.tensor_scalar(out=h_[:], in0=Hacc[:, c, :], scalar1=ss[:, c:c + 1], scalar2=None, op0=ALU.mult),
                 reads=[("H", c), "ss"], writes=[kh])
            for hf in range(2):
                ps, pk = self.next_ps()
                P.op("pe", lambda e, ps=ps, h_=h_, hf=hf: e.transpose(out=ps[:, 0:128], in_=h_[:, hf * 128:(hf + 1) * 128], identity=ident),
                     reads=[kh, "consts"], writes=[pk])
                if col and c >= 2:
                    lc = c - 2
                    o_ap = yrow[hf][:, NCTX:].rearrange("p (r w) -> p w r", w=64)[:, 2 * lc:2 * lc + 2, :]
                    i0 = ps[:, 0:128].rearrange("p (w r) -> p w r", r=64)
                    i1 = SO[:, hf, c * 128:(c + 1) * 128].rearrange("p (w r) -> p w r", r=64)
                else:
                    o_ap = yrow[hf][:, c * 128:(c + 1) * 128]
                    i0 = ps[:, 0:128]
                    i1 = SO[:, hf, c * 128:(c + 1) * 128]
                mn_ap = self.cs("mnorm", hd * 2 + hf)
                P.op("dve", lambda e, o_ap=o_ap, i0=i0, i1=i1, mn_ap=mn_ap: e.scalar_tensor_tensor(
                    out=o_ap, in0=i0, scalar=mn_ap, in1=i1, op0=ALU.mult, op1=ALU.mult),
                    reads=[pk, "SO", "consts"], writes=[f"yrow{hf}"])
        for hf in range(2):
            P.dma("sp", yT[hd * 256 + hf * 128:hd * 256 + (hf + 1) * 128, :], yrow[hf][:], reads=[f"yrow{hf}"])
    P.pop()


Builder.mlstm_mixer = mlstm_mixer


def build_ml_only(col=False):
    B = Builder()
    P = B.P
    B.setup()
    pT = B.ext_in("pT", [ML_IN, S])
    yT = B.ext_out("yT", [D, S], BF16)
    P.barrier()
    B.mlstm_mixer(pT, yT, col)
    return P.finish()
```

```python
import numpy as np
from contextlib import ExitStack
import concourse.bass as bass
import concourse.mybir as mybir
from concourse.bass_utils import run_bass_kernel_spmd

F32 = mybir.dt.float32
BF16 = mybir.dt.bfloat16
AF = mybir.ActivationFunctionType
ALU = mybir.AluOpType

D = 2048
NCTX = 256
NLAT = 4096
S = NCTX + NLAT
DFF = 8192
KC = D // 128
DEPTH = 4
EPS = 1e-6
TOK_TILES = [(0, 256)] + [(256 + 512 * k, 512) for k in range(8)]
TOK256 = [(256 * k, 256) for k in range(17)]
GDN_IN = 12416
ML_IN = 6176
LRU_IN = 4096

import os as _osg
RAW_ONLY = _osg.environ.get("RAW_ONLY", "0") == "1"
COMPUTE = ("pe", "act", "dve", "pool")
ALLQ = ("pe", "act", "dve", "pool", "sp")


class Prog:
    def __init__(self, nc, n_dma_sems=24):
        self.nc = nc
        self.es = ExitStack()
        self.scopes = []
        self.q = {e: [] for e in ALLQ}
        self.cnt = {e: 0 for e in ALLQ}
        self.sem = {}
        for e in COMPUTE:
            self.sem[e] = self.es.enter_context(nc.semaphore("prog_" + e))
        self.dma_pool = {}
        self.dma_idx = {}
        for qn in ("sp", "act", "pool"):
            self.dma_pool[qn] = [self.es.enter_context(nc.semaphore(f"dma_{qn}_{i}")) for i in range(n_dma_sems)]
            self.dma_idx[qn] = 0
        self.waited = {e: {} for e in ALLQ}
        self.state = {}
        self.n_tiles = 0

    def _es(self):
        return self.scopes[-1] if self.scopes else self.es

    def push(self):
        self.scopes.append(ExitStack())

    def pop(self):
        self.barrier()
        self.scopes.pop().close()

    def sbuf(self, name, shape, dtype=F32):
        self.n_tiles += 1
        return self._es().enter_context(self.nc.sbuf_tensor(f"{name}_{self.n_tiles}", list(shape), dtype))

    def psum(self, name, shape, dtype=F32):
        self.n_tiles += 1
        return self._es().enter_context(self.nc.psum_tensor(f"{name}_{self.n_tiles}", list(shape), dtype))

    def dram(self, name, shape, dtype=F32):
        return self.nc.dram_tensor(name, list(shape), dtype, kind="Internal").ap()

    def _deps(self, eng, reads, writes):
        deps = []
        for k in reads:
            st = self.state.get(k)
            if st and st["w"] is not None:
                deps.append((st["w"], "raw"))
        for k in writes:
            st = self.state.get(k)
            if st:
                if st["w"] is not None:
                    deps.append((st["w"], "waw"))
                for t in st["r"].values():
                    deps.append((t, "war"))
        need = {}
        for (tok, kind) in deps:
            semkey, semh, val, own = tok
            if own == eng and own in COMPUTE:
                if eng == "pe" or (kind != "raw" and RAW_ONLY):
                    continue
            if self.waited[eng].get(semkey, 0) >= val:
                continue
            if semkey not in need or need[semkey][1] < val:
                need[semkey] = (semh, val)
        for semkey, (semh, val) in need.items():
            self.waited[eng][semkey] = val
        return list(need.values())

    def _update(self, tok, reads, writes):
        for k in writes:
            self.state[k] = {"w": tok, "r": {}}
        for k in reads:
            if k in writes:
                continue
            st = self.state.setdefault(k, {"w": None, "r": {}})
            st["r"][tok[0]] = tok

    def op(self, eng, fn, reads=(), writes=()):
        writes = list(writes) + [k for k in reads if isinstance(k, str) and k.startswith("ps") and k not in writes]
        waits = self._deps(eng, reads, writes)
        self.cnt[eng] += 1
        tok = (eng, self.sem[eng], self.cnt[eng], eng)
        self.q[eng].append((waits, fn, (self.sem[eng], 1)))
        self._update(tok, reads, writes)
        return tok

    def dma(self, qn, out, in_, reads=(), writes=(), **kw):
        waits = self._deps(qn, reads, writes)
        pool = self.dma_pool[qn]
        i = self.dma_idx[qn]
        self.dma_idx[qn] += 1
        semh = pool[i % len(pool)]
        uses = i // len(pool)
        semkey = (qn, i % len(pool))
        if uses > 0 and self.waited[qn].get(semkey, 0) < 16 * uses:
            waits.append((semh, 16 * uses))
            self.waited[qn][semkey] = 16 * uses
        tok = (semkey, semh, 16 * (uses + 1), "dma")

        def fn(e, out=out, in_=in_, kw=kw):
            return e.dma_start(out=out, in_=in_, **kw)

        self.q[qn].append((waits, fn, (semh, 16)))
        self._update(tok, reads, writes)
        return tok

    def _all_tokens(self):
        toks = []
        for e in COMPUTE:
            if self.cnt[e] > 0:
                toks.append((e, self.sem[e], self.cnt[e]))
        for qn in self.dma_pool:
            pool = self.dma_pool[qn]
            n = self.dma_idx[qn]
            for j in range(min(n, len(pool))):
                uses = (n - 1 - j) // len(pool) + 1
                toks.append(((qn, j), pool[j], 16 * uses))
        return toks

    def barrier(self):
        toks = self._all_tokens()
        for e in ALLQ:
            waits = []
            for (semkey, semh, val) in toks:
                if semkey == e:
                    continue
                if self.waited[e].get(semkey, 0) >= val:
                    continue
                self.waited[e][semkey] = val
                waits.append((semh, val))
            if waits:
                self.q[e].append((waits, None, None))
        self.state = {}

    def finish(self):
        nc = self.nc
        self.barrier()
        q = self.q

        def replay(e, items):
            for (waits, fn, inc) in items:
                for (semh, val) in waits:
                    e.wait_ge(semh, val)
                if fn is not None:
                    ins = fn(e)
                    ins.then_inc(inc[0], inc[1])

        with nc.Block() as block:
            @block.tensor
            def _(e):
                replay(e, q["pe"])

            @block.scalar
            def _(e):
                replay(e, q["act"])

            @block.vector
            def _(e):
                replay(e, q["dve"])

            @block.gpsimd
            def _(e):
                replay(e, q["pool"])

            @block.sync
            def _(e):
                replay(e, q["sp"])
        while self.scopes:
            self.scopes.pop().close()
        self.es.close()
        return nc


def const_layout():
    cols = {}
    off = 0

    def add(name, n):
        nonlocal off
        cols[name] = off
        off += n

    add("c", 32)
    add("bmod", 4 * 96)
    add("nmix", 4 * 16)
    add("nff", 4 * 16)
    add("nout", 16)
    add("lconv", 4 * 16)
    add("lconvb", 16)
    add("lbg", 4 * 16)
    add("llam", 2 * 16)
    add("gconv", 2 * 4 * 64)
    add("gnorm", 2)
    add("galog", 2)
    add("gdt", 2)
    add("mnorm", 16)
    add("mgb", 1)
    add("ident", 128)
    add("mask", 4 * 128)
    add("bmask", 9 * 128)
    return cols, off


def fm(v):
    v = np.asarray(v, np.float32)
    lead = v.shape[:-1]
    n = v.shape[-1] // 128
    v = v.reshape(*lead, n, 128)
    v = np.moveaxis(v, -1, 0)
    return np.ascontiguousarray(v.reshape(128, -1))


def make_consts(inp, b):
    cols, ncol = const_layout()
    C = np.zeros((128, ncol), np.float32)

    def put(name, arr):
        arr = np.asarray(arr, np.float32)
        C[: arr.shape[0], cols[name]: cols[name] + arr.shape[1]] = arr

    cc = np.stack([inp["c"][b], inp["c_ctx"]], axis=-1)
    put("c", cc.reshape(16, 128, 2).transpose(1, 0, 2).reshape(128, 32))
    put("bmod", fm(inp["b_mod"]))
    put("nmix", fm(inp["norm_mix"]))
    put("nff", fm(inp["norm_ff"]))
    put("nout", fm(inp["norm_out"]))
    put("lconv", fm(inp["lru_conv"][0]))
    put("lconvb", fm(inp["lru_conv_b"][0]))
    put("lbg", fm(inp["lru_b_gate"][0]))
    put("llam", fm(inp["lru_lambda"][0]))
    put("gconv", fm(inp["gdn_conv"]))
    put("gnorm", np.asarray(inp["gdn_norm"]).T)
    put("galog", np.asarray(inp["gdn_a_log"]).reshape(2, 64).T)
    put("gdt", np.asarray(inp["gdn_dt_bias"]).reshape(2, 64).T)
    put("mnorm", fm(inp["ml_norm"][0]))
    put("mgb", np.asarray(inp["ml_gate_b"][0]).reshape(32, 1))
    put("ident", np.eye(128, dtype=np.float32))
    jj = np.arange(128)[:, None]
    ii = np.arange(128)[None, :]
    NEG = -30000.0
    ms = [np.where(jj <= ii, 0.0, NEG), np.where(jj < ii, 0.0, NEG), np.where(jj >= ii, 0.0, NEG), np.where(jj > ii, 0.0, NEG)]
    put("mask", np.concatenate(ms, axis=1))
    bm = [(jj // 8 == ii // 8).astype(np.float32)]
    for s_ in (8, 16, 32, 64):
        bm.append((((jj // s_) % 2 == 1) & ((ii // s_) == (jj // s_) - 1)).astype(np.float32))
    for s_ in (8, 16, 32, 64):
        bm.append(bm[1 + (8, 16, 32, 64).index(s_)].T.copy())
    put("bmask", np.concatenate(bm, axis=1))
    return C


class Builder:
    def __init__(self, stage="full", layers=(0, 1, 2, 3)):
        self.stage = stage
        self.layers = layers
        nc = bass.Bass("TRN2", target_bir_lowering=False)
        self.nc = nc
        self.P = Prog(nc)
        self.cols, self.ncol = const_layout()

    def ext_in(self, name, shape, dtype=F32):
        return self.nc.dram_tensor(name, list(shape), dtype, kind="ExternalInput").ap()

    def ext_out(self, name, shape, dtype=F32):
        return self.nc.dram_tensor(name, list(shape), dtype, kind="ExternalOutput").ap()

    def setup(self):
        P = self.P
        self.consts_d = self.ext_in("consts", [128, self.ncol])
        self.CS = P.sbuf("consts", [128, self.ncol])
        P.dma("sp", self.CS[:], self.consts_d, writes=["consts"])
        self.PS = [P.psum("ps", [128, 512]) for _ in range(6)]
        self.ps_i = 0
        self.ones_bf = P.sbuf("ones_bf", [128, 128], BF16)
        P.op("dve", lambda e: e.memset(self.ones_bf[:], 1.0), writes=["ones_bf"])
        self.modT = P.sbuf("modT", [128, 4, 96, 2])
        self.Amod = P.sbuf("Amod", [128, 4, 2, 16, 2])
        self.zero_col = P.sbuf("zero_col", [128, 1])
        P.op("dve", lambda e: e.memset(self.zero_col[:], 0.0), writes=["zero_col"])
        self.alt = 0
        self.eps_col = P.sbuf("eps_col", [128, 1])
        P.op("dve", lambda e: e.memset(self.eps_col[:], EPS), writes=["eps"])
        self.one_col = P.sbuf("one_col", [128, 1])
        P.op("dve", lambda e: e.memset(self.one_col[:], 1.0), writes=["one_col"])

    def cs(self, name, idx=0, n=1):
        o = self.cols[name] + idx
        return self.CS[:, o:o + n]

    def next_ps(self):
        i = self.ps_i % len(self.PS)
        self.ps_i += 1
        return self.PS[i], f"ps{i}"

    def alt_eng(self):
        self.alt += 1
        return "act" if self.alt % 2 else "dve"

    def evac_copy(self, eng, out, in_, reads, writes):
        P = self.P
        if eng == "act":
            P.op("act", lambda e: e.activation(out=out, in_=in_, func=AF.Copy), reads=reads, writes=writes)
        else:
            P.op("dve", lambda e: e.tensor_copy(out=out, in_=in_), reads=reads, writes=writes)

    def modulation(self, w_mod):
        P = self.P
        P.push()
        scT = P.sbuf("scT", [128, 32])
        P.op("act", lambda e: e.activation(out=scT[:], in_=self.cs("c", 0, 32), func=AF.Silu),
             reads=["consts"], writes=["scT"])
        wb = [P.sbuf("wmod", [128, 16, 512]) for _ in range(2)]
        n = 0
        for i in range(DEPTH):
            wv = w_mod[i].rearrange("(k p) f -> p k f", p=128)
            for g in range(24):
                wt = wb[n % 2]
                wk = f"wmod{n % 2}"
                n += 1
                P.dma("sp", wt[:], wv[:, :, g * 512:(g + 1) * 512], writes=[wk])
                ps, pk = self.next_ps()
                for j in range(4):
                    for k in range(16):
                        P.op("pe", lambda e, ps=ps, wt=wt, j=j, k=k: e.matmul(
                            ps[:, j * 2:(j + 1) * 2], lhsT=wt[:, k, j * 128:(j + 1) * 128],
                            rhs=scT[:, k * 2:(k + 1) * 2], start=(k == 0), stop=(k == 15)),
                            reads=[wk, "scT"], writes=[pk])
                for j in range(4):
                    f = g * 4 + j
                    P.op("dve", lambda e, ps=ps, j=j, f=f, i=i: e.tensor_scalar(
                        out=self.modT[:, i, f, :], in0=ps[:, j * 2:(j + 1) * 2],
                        scalar1=self.cs("bmod", i * 96 + f), scalar2=None, op0=ALU.add),
                        reads=[pk, "consts"], writes=["modT"])
        for i in range(DEPTH):
            for sub in range(2):
                nm = "nmix" if sub == 0 else "nff"
                sc_split = 1 if sub == 0 else 4
                for k in range(16):
                    P.op("dve", lambda e, i=i, sub=sub, k=k, nm=nm, sc_split=sc_split: e.tensor_scalar(
                        out=self.Amod[:, i, sub, k, :], in0=self.modT[:, i, sc_split * 16 + k, :],
                        scalar1=1.0, scalar2=self.cs(nm, i * 16 + k), op0=ALU.add, op1=ALU.mult),
                        reads=["modT", "consts"], writes=["Amod"])
        P.pop()

    def mod_ap(self, i, split, k, s):
        return self.modT[:, i, split * 16 + k, s:s + 1]

    def norm_mod(self, src, A_fn, B_fn, hT=None, dst=None, dst_off=0):
        P = self.P
        P.push()
        xb = [P.sbuf("xa", [128, 16, 256]) for _ in range(2)]
        sqb = [P.sbuf("sq", [128, 16, 256], BF16) for _ in range(2)]
        rsb = [P.sbuf("rstd", [128, 256]) for _ in range(2)]
        tmpb = [P.sbuf("tmpa", [128, 256]) for _ in range(3)]
        ob = [P.sbuf("oa", [128, 16, 256]) for _ in range(2)] if dst is not None else None
        srcv = src.rearrange("(k p) t -> p k t", p=128)
        for ti, (t0, tl) in enumerate(TOK256):
            s = 1 if t0 < NCTX else 0
            if dst is not None and t0 < dst_off:
                continue
            xt, xk = xb[ti % 2], f"xa{ti % 2}"
            sq, sk = sqb[ti % 2], f"sq{ti % 2}"
            rs, rk = rsb[ti % 2], f"rs{ti % 2}"
            P.dma("sp", xt[:], srcv[:, :, t0:t0 + tl], writes=[xk])
            P.op("act", lambda e, sq=sq, xt=xt: e.activation(out=sq[:], in_=xt[:], func=AF.Square),
                 reads=[xk], writes=[sk])
            ps, pk = self.next_ps()
            for k in range(16):
                P.op("pe", lambda e, ps=ps, sq=sq, k=k: e.matmul(ps[:, 0:256], lhsT=self.ones_bf[:], rhs=sq[:, k, :],
                                                                start=(k == 0), stop=(k == 15)),
                     reads=[sk, "ones_bf"], writes=[pk])
            P.op("act", lambda e, rs=rs, ps=ps: e.activation(out=rs[:], in_=ps[:, 0:256], func=AF.Ln, scale=1.0 / D, bias=EPS_AP(self)),
                 reads=[pk, "eps"], writes=[rk])
            P.op("act", lambda e, rs=rs: e.activation(out=rs[:], in_=rs[:], func=AF.Exp, scale=-0.5),
                 reads=[rk], writes=[rk])
            for k in range(16):
                tmp, tk = tmpb[k % 3], f"tmpa{k % 3}"
                P.op("dve", lambda e, tmp=tmp, xt=xt, k=k, s=s, rs=rs: e.scalar_tensor_tensor(
                    out=tmp[:], in0=xt[:, k, :], scalar=A_fn(k, s), in1=rs[:], op0=ALU.mult, op1=ALU.mult),
                    reads=[xk, rk, "Amod", "consts"], writes=[tk])
                if hT is not None:
                    o_ap = hT[:, k, t0:t0 + tl]
                    ok = ("hT", k, t0 // 256)
                else:
                    o_ap = ob[ti % 2][:, k, :]
                    ok = f"oa{ti % 2}"
                P.op("act", lambda e, o_ap=o_ap, tmp=tmp, k=k, s=s: e.activation(
                    out=o_ap, in_=tmp[:], func=AF.Identity, bias=B_fn(k, s), scale=1.0),
                    reads=[tk, "modT", "zero_col"], writes=[ok])
            if dst is not None:
                dv = dst.rearrange("(k p) t -> p k t", p=128)
                P.dma("sp", dv[:, :, t0 - dst_off:t0 - dst_off + tl], ob[ti % 2][:], reads=[f"oa{ti % 2}"])
        P.pop()

    def hT_keys(self, k, t0, tl):
        return [("hT", k, j) for j in range(t0 // 256, (t0 + tl) // 256)]

    def linear_hT(self, w, F, hT, evac, row_done):
        P = self.P
        wb = [P.sbuf("wlin", [128, 16, 128], BF16) for _ in range(3)]
        wv = w.rearrange("(k p) f -> p k f", p=128)
        nj = (F + 127) // 128
        for j in range(nj):
            M = min(128, F - j * 128)
            wt, wk = wb[j % 3], f"wlin{j % 3}"
            P.dma("pool", wt[:, :, 0:M], wv[:, :, j * 128:j * 128 + M], writes=[wk])
            for ti, (t0, tl) in enumerate(TOK_TILES):
                ps, pk = self.next_ps()
                for k in range(16):
                    P.op("pe", lambda e, ps=ps, wt=wt, k=k, M=M, t0=t0, tl=tl: e.matmul(
                        ps[0:M, 0:tl], lhsT=wt[:, k, 0:M], rhs=hT[:, k, t0:t0 + tl], start=(k == 0), stop=(k == 15)),
                        reads=[wk] + self.hT_keys(k, t0, tl), writes=[pk])
                evac(j, M, ti, t0, tl, ps, pk)
            row_done(j, M)

    def in_proj(self, w, F, hT, pT, col):
        P = self.P
        stg = [P.sbuf("stg", [128, S]) for _ in range(2)]

        def evac(j, M, ti, t0, tl, ps, pk):
            st = stg[j % 2]
            if col and t0 >= NCTX:
                r0 = (t0 - NCTX) // 64
                o_ap = st[0:M, NCTX:].rearrange("p (w r) -> p r w", r=64)[:, r0:r0 + 8, :]
                i_ap = ps[0:M, 0:512].rearrange("p (r w) -> p r w", w=64)
            else:
                o_ap = st[0:M, t0:t0 + tl]
                i_ap = ps[0:M, 0:tl]
            self.evac_copy(self.alt_eng(), o_ap, i_ap, [pk], [(f"stg{j % 2}", ti)])

        def row_done(j, M):
            st = stg[j % 2]
            P.dma("sp", pT[j * 128:j * 128 + M, :], st[0:M, :], reads=[(f"stg{j % 2}", ti) for ti in range(9)])
            for ti in range(9):
                pass

        self.linear_hT(w, F, hT, evac, row_done)

    def out_proj(self, yT, w, Kdim, G_fn, res, dump=None):
        P = self.P
        P.push()
        kc = Kdim // 128
        W = P.sbuf("wout", [128, kc, D], BF16)
        wv = w.rearrange("(k p) d -> p k d", p=128)
        for k0 in range(0, kc, 4):
            P.dma("pool", W[:, k0:k0 + 4, :], wv[:, k0:k0 + 4, :], writes=[("wout", k0)])
        yb = [P.sbuf("yb", [128, kc, 256], BF16) for _ in range(2)]
        nrb = 1 if kc > 16 else 2
        rb = [P.sbuf("rb", [128, 16, 256]) for _ in range(nrb)]
        yv = yT.rearrange("(k p) t -> p k t", p=128)
        rv = res.rearrange("(k p) t -> p k t", p=128) if res is not None else None
        dvv = dump.rearrange("(k p) t -> p k t", p=128) if dump is not None else None
        for ti, (t0, tl) in enumerate(TOK256):
            s = 1 if t0 < NCTX else 0
            yt, yk = yb[ti % 2], f"yb{ti % 2}"
            rt, rk = rb[ti % nrb], f"rb{ti % nrb}"
            P.dma("sp", yt[:], yv[:, :, t0:t0 + tl], writes=[yk])
            if dump is None:
                P.dma("sp", rt[:], rv[:, :, t0:t0 + tl], writes=[rk])
            for d in range(16):
                ps, pk = self.next_ps()
                for k in range(kc):
                    P.op("pe", lambda e, ps=ps, k=k, d=d, yt=yt: e.matmul(
                        ps[:, 0:256], lhsT=W[:, k, d * 128:(d + 1) * 128], rhs=yt[:, k, :], start=(k == 0), stop=(k == kc - 1)),
                        reads=[("wout", (k // 4) * 4), yk], writes=[pk])
                if dump is None:
                    P.op("dve", lambda e, ps=ps, rt=rt, d=d, s=s: e.scalar_tensor_tensor(
                        out=rt[:, d, :], in0=ps[:, 0:256], scalar=G_fn(d, s), in1=rt[:, d, :], op0=ALU.mult, op1=ALU.add),
                        reads=[pk, rk, "modT"], writes=[rk])
                else:
                    self.evac_copy(self.alt_eng(), rt[:, d, :], ps[:, 0:256], [pk], [rk])
            if dump is None:
                P.dma("sp", rv[:, :, t0:t0 + tl], rt[:], reads=[rk])
            else:
                P.dma("sp", dvv[:, :, t0:t0 + tl], rt[:], reads=[rk])
        P.pop()

    def ffn(self, i, w_up, w_down, res, aT):
        P = self.P
        P.push()
        hT = P.sbuf("hT", [128, 16, S], BF16)
        self.norm_mod(res, lambda k, s: self.Amod[:, i, 1, k, s:s + 1], lambda k, s: self.mod_ap(i, 3, k, s), hT=hT)
        stg = [P.sbuf("stgb", [128, S], BF16) for _ in range(2)]
        tmpr = [P.sbuf("tmpr", [128, 512]) for _ in range(3)]
        cnt = [0]

        def evac(j, M, ti, t0, tl, ps, pk):
            st = stg[j % 2]
            n = cnt[0] % 3
            cnt[0] += 1
            tmp = tmpr[n]
            P.op("act", lambda e, tmp=tmp, ps=ps, tl=tl: e.activation(out=tmp[:, 0:tl], in_=ps[:, 0:tl], func=AF.Relu),
                 reads=[pk], writes=[f"tmpr{n}"])
            P.op("dve", lambda e, st=st, tmp=tmp, t0=t0, tl=tl: e.tensor_tensor(
                out=st[:, t0:t0 + tl], in0=tmp[:, 0:tl], in1=tmp[:, 0:tl], op=ALU.mult),
                reads=[f"tmpr{n}"], writes=[(f"stgb{j % 2}", ti)])

        def row_done(j, M):
            P.dma("sp", aT[j * 128:(j + 1) * 128, :], stg[j % 2][:], reads=[(f"stgb{j % 2}", ti) for ti in range(9)])

        self.linear_hT(w_up, DFF, hT, evac, row_done)
        P.pop()
        P.push()
        ab = P.sbuf("ab", [128, 64, 1024], BF16)
        wdb = [P.sbuf("wd", [128, 64, 128], BF16) for _ in range(2)]
        rp = [P.sbuf("rp", [128, 512]) for _ in range(4)]
        av = aT.rearrange("(k p) t -> p k t", p=128)
        wv = w_down.rearrange("(k p) d -> p k d", p=128)
        blocks = [(0, 256)] + [(256 + 1024 * k, 1024) for k in range(4)]
        nw = 0
        nr = 0
        for (b0, bl) in blocks:
            s = 1 if b0 < NCTX else 0
            for k0 in range(0, 64, 16):
                P.dma("sp", ab[:, k0:k0 + 16, 0:bl], av[:, k0:k0 + 16, b0:b0 + bl],
                      writes=[("ab", k0)])
            for d in range(16):
                wt, wk = wdb[nw % 2], f"wd{nw % 2}"
                nw += 1
                P.dma("pool", wt[:], wv[:, :, d * 128:(d + 1) * 128], writes=[wk])
                for c0 in range(0, bl, 512):
                    cl = min(512, bl - c0)
                    r, rk = rp[nr % 4], f"rp{nr % 4}"
                    nr += 1
                    P.dma("sp", r[:, 0:cl], res[d * 128:(d + 1) * 128, b0 + c0:b0 + c0 + cl], writes=[rk])
                    ps, pk = self.next_ps()
                    for k in range(64):
                        P.op("pe", lambda e, ps=ps, wt=wt, k=k, c0=c0, cl=cl: e.matmul(
                            ps[:, 0:cl], lhsT=wt[:, k, :], rhs=ab[:, k, c0:c0 + cl], start=(k == 0), stop=(k == 63)),
                            reads=[wk, ("ab", (k // 16) * 16)], writes=[pk])
                    P.op("dve", lambda e, ps=ps, r=r, cl=cl, d=d, s=s: e.scalar_tensor_tensor(
                        out=r[:, 0:cl], in0=ps[:, 0:cl], scalar=self.mod_ap(i, 5, d, s), in1=r[:, 0:cl],
                        op0=ALU.mult, op1=ALU.add), reads=[pk, rk, "modT"], writes=[rk])
                    P.dma("sp", res[d * 128:(d + 1) * 128, b0 + c0:b0 + c0 + cl], r[:, 0:cl], reads=[rk])
        P.pop()

    def lru_mixer(self, pT, yT, w_gate):
        P = self.P
        P.push()
        xr = [P.sbuf("xr", [128, S]) for _ in range(2)]
        xrb = [P.sbuf("xrb", [128, S], BF16) for _ in range(2)]
        wg = P.sbuf("wg", [128, 2, 4, 256], BF16)
        T = [P.sbuf("lt", [128, S]) for _ in range(5)]
        ybf = P.sbuf("ybf", [128, S], BF16)
        ncl = P.sbuf("ncl", [128, 32])
        P.op("act", lambda e: e.activation(out=ncl[:], in_=self.cs("llam", 0, 32), func=AF.Exp, scale=-1.0),
             reads=["consts"], writes=["ncl"])
        P.op("act", lambda e: e.activation(out=ncl[:], in_=ncl[:], func=AF.Ln, scale=1.0, bias=self.one_col[:]),
             reads=["ncl", "one_col"], writes=["ncl"])
        P.op("dve", lambda e: e.tensor_scalar(out=ncl[:], in0=ncl[:], scalar1=-8.0, scalar2=None, op0=ALU.mult),
             reads=["ncl"], writes=["ncl"])
        segs = [(0, NCTX), (NCTX, S)]
        for n in range(8):
            for dg in range(4):
                P.dma("pool", wg[:, :, dg, :], w_gate[dg // 2, dg % 2, n].rearrange("(kc p) j -> p kc j", p=128),
                      writes=["wg"])
            for kc in range(2):
                c = 2 * n + kc
                px, pxk = T[4], "lt4"
                P.dma("sp", px[:], pT[D + c * 128:D + (c + 1) * 128, :], writes=[pxk])
                x_, xk = xr[kc], f"xr{kc}"
                P.op("dve", lambda e, x_=x_, px=px, c=c: e.tensor_scalar(
                    out=x_[:], in0=px[:], scalar1=self.cs("lconv", 2 * 16 + c), scalar2=self.cs("lconvb", c),
                    op0=ALU.mult, op1=ALU.add), reads=[pxk, "consts"], writes=[xk])
                for (a, b_) in segs:
                    for tap, sh in ((0, -2), (1, -1), (3, 1)):
                        lo = max(a, a - sh)
                        hi = min(b_, b_ - sh)
                        P.op("dve", lambda e, x_=x_, px=px, c=c, tap=tap, sh=sh, lo=lo, hi=hi: e.scalar_tensor_tensor(
                            out=x_[:, lo:hi], in0=px[:, lo + sh:hi + sh], scalar=self.cs("lconv", tap * 16 + c),
                            in1=x_[:, lo:hi], op0=ALU.mult, op1=ALU.add), reads=[pxk, xk, "consts"], writes=[xk])
                P.op("pool", lambda e, kc=kc, x_=x_: e.tensor_copy(out=xrb[kc][:], in_=x_[:]), reads=[xk], writes=[f"xrb{kc}"])
            for jc in range(2):
                cj = 2 * n + jc
                for dr in range(2):
                    for gate in range(2):
                        Tg, tgk = (T[0], "lt0") if gate == 0 else (T[1], "lt1")
                        for ti, (t0, tl) in enumerate(TOK_TILES):
                            ps, pk = self.next_ps()
                            for kc in range(2):
                                P.op("pe", lambda e, ps=ps, kc=kc, dr=dr, gate=gate, jc=jc, t0=t0, tl=tl: e.matmul(
                                    ps[:, 0:tl], lhsT=wg[:, kc, dr * 2 + gate, jc * 128:(jc + 1) * 128],
                                    rhs=xrb[kc][:, t0:t0 + tl], start=(kc == 0), stop=(kc == 1)),
                                    reads=["wg", f"xrb{kc}"], writes=[pk])
                            P.op("act", lambda e, Tg=Tg, ps=ps, t0=t0, tl=tl, dr=dr, gate=gate, cj=cj: e.activation(
                                out=Tg[:, t0:t0 + tl], in_=ps[:, 0:tl], func=AF.Sigmoid,
                                bias=self.cs("lbg", (dr * 2 + gate) * 16 + cj), scale=1.0),
                                reads=[pk, "consts"], writes=[tgk])
                    P.op("act", lambda e, dr=dr, cj=cj: e.activation(out=T[0][:], in_=T[0][:], func=AF.Exp,
                                                                     scale=ncl[:, dr * 16 + cj:dr * 16 + cj + 1]),
                         reads=["lt0", "ncl"], writes=["lt0"])
                    P.op("dve", lambda e: e.tensor_tensor(out=T[2][:], in0=T[0][:], in1=T[0][:], op=ALU.mult),
                         reads=["lt0"], writes=["lt2"])
                    P.op("dve", lambda e: e.tensor_scalar(out=T[2][:], in0=T[2][:], scalar1=-1.0, scalar2=1.0,
                                                          op0=ALU.mult, op1=ALU.add), reads=["lt2"], writes=["lt2"])
                    P.op("dve", lambda e: e.tensor_scalar(out=T[2][:], in0=T[2][:], scalar1=0.0, scalar2=None, op0=ALU.max),
                         reads=["lt2"], writes=["lt2"])
                    P.op("act", lambda e: e.activation(out=T[2][:], in_=T[2][:], func=AF.Sqrt), reads=["lt2"], writes=["lt2"])
                    P.op("dve", lambda e: e.tensor_tensor(out=T[1][:], in0=T[1][:], in1=T[2][:], op=ALU.mult),
                         reads=["lt1", "lt2"], writes=["lt1"])
                    P.op("dve", lambda e, jc=jc: e.tensor_tensor(out=T[1][:], in0=T[1][:], in1=xr[jc][:], op=ALU.mult),
                         reads=["lt1", f"xr{jc}"], writes=["lt1"])
                    if dr == 0:
                        P.op("dve", lambda e: e.tensor_tensor_scan(out=T[3][:], data0=T[0][:], data1=T[1][:], initial=0.0,
                                                                   op0=ALU.mult, op1=ALU.add),
                             reads=["lt0", "lt1"], writes=["lt3"])
                    else:
                        P.op("dve", lambda e: e.tensor_tensor_scan(
                            out=T[2][:, NCTX - 1::-1], data0=T[0][:, NCTX - 1::-1], data1=T[1][:, NCTX - 1::-1],
                            initial=0.0, op0=ALU.mult, op1=ALU.add), reads=["lt0", "lt1"], writes=["lt2"])
                        P.op("dve", lambda e: e.tensor_tensor_scan(
                            out=T[2][:, S - 1:NCTX - 1:-1], data0=T[0][:, S - 1:NCTX - 1:-1], data1=T[1][:, S - 1:NCTX - 1:-1],
                            initial=T[2][:, 0:1], op0=ALU.mult, op1=ALU.add), reads=["lt0", "lt1", "lt2"], writes=["lt2"])
                        P.op("dve", lambda e: e.tensor_tensor(out=T[3][:], in0=T[3][:], in1=T[2][:], op=ALU.add),
                             reads=["lt3", "lt2"], writes=["lt3"])
                py, pyk = T[4], "lt4"
                P.dma("sp", py[:], pT[cj * 128:(cj + 1) * 128, :], writes=[pyk])
                P.op("act", lambda e: e.activation(out=T[2][:], in_=T[4][:], func=AF.Square), reads=["lt4"], writes=["lt2"])
                P.op("dve", lambda e: e.tensor_scalar(out=T[2][:], in0=T[2][:], scalar1=0.044715, scalar2=1.0,
                                                      op0=ALU.mult, op1=ALU.add), reads=["lt2"], writes=["lt2"])
                P.op("dve", lambda e: e.tensor_tensor(out=T[2][:], in0=T[2][:], in1=T[4][:], op=ALU.mult),
                     reads=["lt2", "lt4"], writes=["lt2"])
                P.op("act", lambda e: e.activation(out=T[2][:], in_=T[2][:], func=AF.Sigmoid, scale=1.5957691216057308),
                     reads=["lt2"], writes=["lt2"])
                P.op("dve", lambda e: e.tensor_tensor(out=T[2][:], in0=T[2][:], in1=T[4][:], op=ALU.mult),
                     reads=["lt2", "lt4"], writes=["lt2"])
                P.op("dve", lambda e: e.tensor_tensor(out=ybf[:], in0=T[2][:], in1=T[3][:], op=ALU.mult),
                     reads=["lt2", "lt3"], writes=["ybf"])
                P.dma("sp", yT[cj * 128:(cj + 1) * 128, :], ybf[:], reads=["ybf"])
        P.pop()


def EPS_AP(self):
    return self.eps_col[:]


def build_mixer_test(kind, col=False, lj=0):
    B = Builder()
    P = B.P
    B.setup()
    hin = B.ext_in("hin", [D, S])
    yout = B.ext_out("yout", [D, S])
    if kind == "gdn":
        w_in = B.ext_in("w_in", [D, GDN_IN])
        w_out = B.ext_in("w_out", [2 * D, D])
        F, VW = GDN_IN, 2 * D
    if kind == "ml":
        w_in = B.ext_in("w_in", [D, ML_IN])
        w_out = B.ext_in("w_out", [D, D])
        F, VW = ML_IN, D
    if kind == "lru":
        w_in = B.ext_in("w_in", [D, LRU_IN])
        w_gate = B.ext_in("w_gate", [2, 2, 8, 256, 256])
        w_out = B.ext_in("w_out", [D, D])
        F, VW = LRU_IN, D
    pT = P.dram("pT", [F, S])
    yT = P.dram("yT", [VW, S], BF16)
    P.push()
    hT = P.sbuf("hT", [128, 16, S], BF16)
    P.dma("pool", hT[:], hin.rearrange("(k p) t -> p k t", p=128), writes=["hT_all"])
    P.barrier()
    B.in_proj(w_in, F, hT, pT, col)
    P.pop()
    if kind == "lru":
        B.lru_mixer(pT, yT, w_gate)
    if kind == "ml":
        B.mlstm_mixer(pT, yT, col)
    import os as _os
    if kind == "gdn" and int(_os.environ.get("GDN_ST", "9")) > 0:
        B.gdn_mixer(lj, pT, yT, col)
    B.out_proj(yT, w_out, VW, None, None, dump=yout)
    return P.finish()


def conv_silu_row(self, dst, dkey, px, pxk, src_rows, wcol_fn, silu=True):
    P = self.P
    P.dma("sp", px, src_rows, writes=[pxk])
    P.op("dve", lambda e: e.tensor_scalar(out=dst[:], in0=px, scalar1=wcol_fn(2), scalar2=None, op0=ALU.mult),
         reads=[pxk, "consts"], writes=[dkey])
    for (a, b_) in ((0, NCTX), (NCTX, S)):
        for tap, sh in ((0, -2), (1, -1), (3, 1)):
            lo = max(a, a - sh)
            hi = min(b_, b_ - sh)
            P.op("dve", lambda e, tap=tap, sh=sh, lo=lo, hi=hi: e.scalar_tensor_tensor(
                out=dst[:, lo:hi], in0=px[:, lo + sh:hi + sh], scalar=wcol_fn(tap), in1=dst[:, lo:hi],
                op0=ALU.mult, op1=ALU.add), reads=[pxk, dkey, "consts"], writes=[dkey])
    if silu:
        P.op("act", lambda e: e.activation(out=dst[:], in_=dst[:], func=AF.Silu), reads=[dkey], writes=[dkey])


def gdn_mixer(self, lj, pT, yT, col):
    import os as _os2
    P = self.P
    P.push()
    NCH = 34
    ident = self.cs("ident", 0, 128)

    def mask_ap(dr, strict):
        return self.cs("mask", (dr * 2 + strict) * 128, 128)

    onesf = P.sbuf("onesf", [128, 128])
    P.op("dve", lambda e: e.memset(onesf[:], 1.0), writes=["onesf"])
    GG = P.sbuf("GG", [64, S])
    negGT = P.sbuf("negGT", [128, NCH, 64])
    GT = P.sbuf("GT", [128, NCH, 64])
    eGLBT = P.sbuf("eGLBT", [128, NCH, 64])
    betaT = P.sbuf("betaT", [128, NCH, 64])
    negident = P.sbuf("negident", [128, 128])
    P.op("dve", lambda e: e.tensor_scalar(out=negident[:], in0=ident, scalar1=-1.0, scalar2=None, op0=ALU.mult),
         reads=["consts"], writes=["negident"])
    P.push()
    LB = P.sbuf("LB", [64, S])
    DL = P.sbuf("DL", [64, S])
    nea = P.sbuf("nea", [64, 1])
    tri = [P.sbuf("tri", [128, 128]) for _ in range(2)]
    dltb = [P.sbuf("dlt", [128, 128]) for _ in range(2)]
    glbb = [P.sbuf("glb", [128, 64]) for _ in range(2)]
    for dr in range(2):
        P.op("dve", lambda e, dr=dr: e.tensor_scalar(out=tri[dr][:], in0=mask_ap(dr, 0), scalar1=1.0 / 30000.0, scalar2=1.0,
                                                     op0=ALU.mult, op1=ALU.add), reads=["consts"], writes=[f"tri{dr}"])
    P.dma("sp", LB[:], pT[12288:12352, :], writes=["LB"])
    P.dma("sp", DL[:], pT[12352:12416, :], writes=["DL"])
    P.op("act", lambda e: e.activation(out=LB[:], in_=LB[:], func=AF.Exp, scale=-1.0), reads=["LB"], writes=["LB"])
    P.op("act", lambda e: e.activation(out=LB[:], in_=LB[:], func=AF.Ln, bias=self.one_col[0:64, :], scale=1.0),
         reads=["LB", "one_col"], writes=["LB"])
    P.op("dve", lambda e: e.tensor_scalar(out=LB[:], in0=LB[:], scalar1=-1.0, scalar2=None, op0=ALU.mult),
         reads=["LB"], writes=["LB"])
    P.op("act", lambda e: e.activation(out=DL[:], in_=DL[:], func=AF.Exp, bias=self.cs("gdt", lj)[0:64, :], scale=1.0),
         reads=["DL", "consts"], writes=["DL"])
    P.op("act", lambda e: e.activation(out=DL[:], in_=DL[:], func=AF.Ln, bias=self.one_col[0:64, :], scale=1.0),
         reads=["DL", "one_col"], writes=["DL"])
    P.op("act", lambda e: e.activation(out=nea[:], in_=self.cs("galog", lj)[0:64, :], func=AF.Exp), reads=["consts"], writes=["nea"])
    P.op("dve", lambda e: e.tensor_scalar(out=nea[:], in0=nea[:], scalar1=-1.0, scalar2=None, op0=ALU.mult), reads=["nea"], writes=["nea"])
    P.op("dve", lambda e: e.tensor_scalar(out=DL[:], in0=DL[:], scalar1=nea[:, 0:1], scalar2=None, op0=ALU.mult),
         reads=["DL", "nea"], writes=["DL"])
    for c in range(NCH):
        cs_ = slice(c * 128, (c + 1) * 128)
        dlt, kd_ = dltb[c % 2], f"dlt{c % 2}"
        glb, kg_ = glbb[c % 2], f"glb{c % 2}"
        ps, pk = self.next_ps()
        P.op("pe", lambda e, ps=ps, cs_=cs_: e.transpose(out=ps[:, 0:64], in_=DL[:, cs_], identity=ident[0:64, 0:64]),
             reads=["DL", "consts"], writes=[pk])
        P.op("pe", lambda e, ps=ps, cs_=cs_: e.transpose(out=ps[:, 64:128], in_=LB[:, cs_], identity=ident[0:64, 0:64]),
             reads=["LB", "consts"], writes=[pk])
        P.op("act", lambda e, ps=ps, dlt=dlt: e.activation(out=dlt[:], in_=ps[:, 0:128], func=AF.Copy), reads=[pk], writes=[kd_])
        ps2, pk2 = self.next_ps()
        P.op("pe", lambda e, ps2=ps2, dlt=dlt: e.matmul(ps2[:, 0:32], lhsT=tri[0][:], rhs=dlt[:, 0:32], start=True, stop=True),
             reads=["tri0", kd_], writes=[pk2])
        P.op("pe", lambda e, ps2=ps2, dlt=dlt: e.matmul(ps2[:, 32:64], lhsT=tri[1][:], rhs=dlt[:, 32:64], start=True, stop=True),
             reads=["tri1", kd_], writes=[pk2])
        P.op("act", lambda e, ps2=ps2, c=c: e.activation(out=GT[:, c, :], in_=ps2[:, 0:64], func=AF.Copy), reads=[pk2], writes=[("GT", c)])
        P.op("dve", lambda e, ps2=ps2, c=c: e.tensor_scalar(out=negGT[:, c, :], in0=ps2[:, 0:64], scalar1=-1.0, scalar2=None, op0=ALU.mult),
             reads=[pk2], writes=["negGT"])
        P.op("dve", lambda e, ps2=ps2, dlt=dlt, glb=glb: e.tensor_tensor(out=glb[:], in0=ps2[:, 0:64], in1=dlt[:, 64:128], op=ALU.add),
             reads=[pk2, kd_], writes=[kg_])
        P.op("act", lambda e, glb=glb, c=c: e.activation(out=eGLBT[:, c, :], in_=glb[:], func=AF.Exp), reads=[kg_], writes=["eGLBT"])
        P.op("act", lambda e, dlt=dlt, c=c: e.activation(out=betaT[:, c, :], in_=dlt[:, 64:128], func=AF.Exp), reads=[kd_], writes=["betaT"])
        ps3, pk3 = self.next_ps()
        P.op("pe", lambda e, ps3=ps3, c=c: e.transpose(out=ps3[0:64, 0:128], in_=GT[:, c, :], identity=ident), reads=[("GT", c), "consts"], writes=[pk3])
        P.op("dve", lambda e, ps3=ps3, cs_=cs_: e.tensor_copy(out=GG[:, cs_], in_=ps3[0:64, 0:128]), reads=[pk3], writes=["GG"])
    P.pop()
    QT = P.sbuf("QT", [128, S])
    KT = P.sbuf("KT", [128, S])
    VT = P.sbuf("VT", [128, S])
    SZ = P.sbuf("SZ", [128, S], BF16)
    HC = 9
    Ust = P.sbuf("Ust", [128, HC, 128])
    Wst = P.sbuf("Wst", [128, HC, 128])
    Ast = P.sbuf("Ast", [128, HC, 128])
    Kst = P.sbuf("Kst", [128, HC, 128])
    Qst = P.sbuf("Qst", [128, HC, 128])
    EGL = P.sbuf("EGL", [128, HC])
    Oacc = P.sbuf("Oacc", [128, NCH, 128])
    Oflat = Oacc[:].rearrange("p c e -> p (c e)")
    yrow = P.sbuf("yrow", [128, S], BF16)
    Sst = [P.sbuf("Sst", [128, 128]) for _ in range(2)]
    NT = 24
    tmps = [P.sbuf("tm", [128, 128]) for _ in range(NT)]
    tcnt = [0]

    def tmp():
        i = tcnt[0] % NT
        tcnt[0] += 1
        return tmps[i], f"tm{i}"

    tgs = [P.sbuf("tg", [64, 128]) for _ in range(2)]
    tgi = [0]
    roles = {nm: [P.sbuf("rl_" + nm, [128, 128]) for _ in range(2)] for nm in ("Nt", "N_", "X", "Vb")}

    def role(nm, i):
        return roles[nm][i], f"rl_{nm}{i}"

    cols_ = [P.sbuf("cl", [128, 1]) for _ in range(8)]
    ccnt = [0]

    def colt():
        i = ccnt[0] % 8
        ccnt[0] += 1
        return cols_[i], f"cl{i}"

    ss = P.sbuf("ss", [128, NCH])
    dbgt = P.sbuf("dbgt", [128, 128])
    rsq = [P.sbuf("rsq", [128, 512]) for _ in range(2)]

    def l2norm_row(X, xk, scale):
        for ti, (t0, tl) in enumerate(TOK_TILES):
            sq, sk = rsq[ti % 2], f"rsq{ti % 2}"
            P.op("act", lambda e, sq=sq, t0=t0, tl=tl: e.activation(out=sq[:, 0:tl], in_=X[:, t0:t0 + tl], func=AF.Square),
                 reads=[xk], writes=[sk])
            ps, pk = self.next_ps()
            P.op("pe", lambda e, ps=ps, sq=sq, tl=tl: e.matmul(ps[:, 0:tl], lhsT=onesf[:], rhs=sq[:, 0:tl], start=True, stop=True),
                 reads=[sk, "onesf"], writes=[pk])
            P.op("act", lambda e, ps=ps, sq=sq, tl=tl: e.activation(out=sq[:, 0:tl], in_=ps[:, 0:tl], func=AF.Ln, bias=self.eps_col[:], scale=1.0),
                 reads=[pk, "eps"], writes=[sk])
            P.op("act", lambda e, sq=sq, tl=tl: e.activation(out=sq[:, 0:tl], in_=sq[:, 0:tl], func=AF.Exp, scale=-0.5),
                 reads=[sk], writes=[sk])
            P.op("dve", lambda e, sq=sq, t0=t0, tl=tl: e.scalar_tensor_tensor(
                out=X[:, t0:t0 + tl], in0=X[:, t0:t0 + tl], scalar=scale, in1=sq[:, 0:tl], op0=ALU.mult, op1=ALU.mult),
                reads=[xk, sk], writes=[xk])

    def gcol(chunk):
        return lambda tap: self.cs("gconv", (lj * 4 + tap) * 64 + chunk)

    order_f = list(range(NCH))
    order_b = [1, 0] + list(range(NCH - 1, 1, -1))

    DBG_U = int(_os2.environ.get("GDN_U", "100000"))
    ucnt = [0]

    def uop(*a_, **k_):
        ucnt[0] += 1
        if ucnt[0] > DBG_U:
            return None
        return P.op(*a_, **k_)

    def unit(vh, dr, sl, c):
        ucnt[0] = 0
        r = dr * 32 + vh
        cs_ = slice(c * 128, (c + 1) * 128)
        last = 127 if dr == 0 else 0
        tgi[0] += 1
        par = tgi[0] % 2
        t_g, kg = tgs[par], f"tg{par}"
        uop("dve", lambda e: e.tensor_scalar(out=t_g[0:64, :], in0=GG[0:64, cs_], scalar1=ident[0:64, r:r + 1], scalar2=None, op0=ALU.mult),
             reads=["GG", "consts"], writes=[kg])
        psA, kA = self.next_ps()
        uop("pe", lambda e: e.matmul(psA[:, 0:128], lhsT=onesf[0:64, :], rhs=t_g[0:64, :], start=True, stop=False), reads=[kg, "onesf"], writes=[kA])
        uop("pe", lambda e: e.matmul(psA[:, 0:128], lhsT=ident, rhs=mask_ap(dr, 0), start=False, stop=True), reads=["consts"], writes=[kA])
        uop("pe", lambda e: e.matmul(psA[:, 128:256], lhsT=onesf[0:64, :], rhs=t_g[0:64, :], start=True, stop=False), reads=[kg, "onesf"], writes=[kA])
        uop("pe", lambda e: e.matmul(psA[:, 128:256], lhsT=negident[:], rhs=mask_ap(1 - dr, 1), start=False, stop=True), reads=["consts", "negident"], writes=[kA])
        uop("pe", lambda e: e.matmul(psA[:, 256:384], lhsT=onesf[0:64, :], rhs=t_g[0:64, :], start=True, stop=True), reads=[kg, "onesf"], writes=[kA])
        ngc = negGT[:, c, r:r + 1]
        Dm, kDm = tmp()
        uop("act", lambda e: e.activation(out=Dm[:], in_=psA[:, 0:128], func=AF.Exp, bias=ngc, scale=1.0), reads=[kA, "negGT"], writes=[kDm])
        DmB, kDmB = tmp()
        uop("act", lambda e: e.activation(out=DmB[:], in_=psA[:, 128:256], func=AF.Exp, bias=GT[:, c, r:r + 1], scale=-1.0), reads=[kA, ("GT", c)], writes=[kDmB])
        EG, kEG = tmp()
        uop("act", lambda e: e.activation(out=EG[:], in_=psA[:, 256:384], func=AF.Exp), reads=[kA], writes=[kEG])
        glc, kglc = colt()
        uop("act", lambda e: e.activation(out=glc[:], in_=psA[:, 256 + last:256 + last + 1], func=AF.Copy), reads=[kA], writes=[kglc])
        kdc, kkdc = colt()
        uop("act", lambda e: e.activation(out=kdc[:], in_=ngc, func=AF.Exp, bias=glc[:], scale=1.0), reads=[kglc, "negGT"], writes=[kkdc])
        uop("dve", lambda e: e.tensor_tensor(out=Qst[:, sl, :], in0=QT[:, cs_], in1=EG[:], op=ALU.mult), reads=["QT", kEG], writes=[("Q", sl)])
        uop("dve", lambda e: e.tensor_copy(out=EGL[:, sl:sl + 1], in_=EG[:, last:last + 1]), reads=[kEG], writes=[("EGL", sl)])
        yield
        psK, kK = self.next_ps()
        uop("pe", lambda e: e.matmul(psK[:, 0:128], lhsT=KT[:, cs_], rhs=KT[:, cs_], start=True, stop=True), reads=["KT"], writes=[kK])
        uop("pe", lambda e: e.matmul(psK[:, 128:256], lhsT=KT[:, cs_], rhs=QT[:, cs_], start=True, stop=True), reads=["KT", "QT"], writes=[kK])
        uop("dve", lambda e: e.tensor_tensor(out=Ast[:, sl, :], in0=psK[:, 128:256], in1=Dm[:], op=ALU.mult), reads=[kK, kDm], writes=[("A", sl)])
        Nt, kNt = role("Nt", par)
        uop("dve", lambda e: e.scalar_tensor_tensor(out=Nt[:], in0=psK[:, 0:128], scalar=betaT[:, c, r:r + 1], in1=DmB[:], op0=ALU.mult, op1=ALU.mult),
             reads=[kK, kDmB, "betaT"], writes=[kNt])
        yield
        psT, kT = self.next_ps()
        uop("pe", lambda e: e.transpose(out=psT[:, 0:128], in_=KT[:, cs_], identity=ident), reads=["KT", "consts"], writes=[kT])
        uop("pe", lambda e: e.transpose(out=psT[:, 128:256], in_=VT[:, cs_], identity=ident), reads=["VT", "consts"], writes=[kT])
        psT2, kT2 = self.next_ps()
        _v = _os2.environ.get("GDN_V", "")
        if _v == "C":
            uop("pe", lambda e: e.transpose(out=psT2[:, 0:128], in_=KT[:, cs_], identity=ident), reads=[kNt, "KT", "consts"], writes=[kT2])
        else:
            uop("pe", lambda e: e.transpose(out=psT2[:, 0:128], in_=Nt[:], identity=ident), reads=[kNt, "consts"], writes=[kT2])
        X, kX = role("X", par)
        uop("dve", lambda e: e.tensor_scalar(out=X[:], in0=psT[:, 0:128], scalar1=eGLBT[:, c, r:r + 1], scalar2=None, op0=ALU.mult),
             reads=[kT, "eGLBT"], writes=[kX])
        uop("dve", lambda e: e.tensor_scalar(out=Kst[:, sl, :], in0=psT[:, 0:128], scalar1=kdc[:, 0:1], scalar2=None, op0=ALU.mult),
             reads=[kT, kkdc], writes=[("K", sl)])
        Vb, kVb = role("Vb", par)
        uop("dve", lambda e: e.tensor_scalar(out=Vb[:], in0=psT[:, 128:256], scalar1=betaT[:, c, r:r + 1], scalar2=None, op0=ALU.mult),
             reads=[kT, "betaT"], writes=[kVb])
        N_, kN = role("N_", par)
        if _v == "B":
            uop("dve", lambda e: e.tensor_copy(out=dbgt[:], in_=psT2[:, 0:128]), reads=[kT2], writes=["dbgt"])
        else:
            uop("dve", lambda e: e.tensor_copy(out=N_[:], in_=psT2[:, 0:128]), reads=[kT2], writes=[kN])
        def bm(idx):
            return self.cs("bmask", idx * 128, 128)

        def mm_to(lhsT, klh, rhs, krh):
            ps_, pk_ = self.next_ps()
            uop("pe", lambda e, ps_=ps_, lhsT=lhsT, rhs=rhs: e.matmul(ps_[:, 0:128], lhsT=lhsT[:], rhs=rhs[:], start=True, stop=True),
                reads=[klh, krh], writes=[pk_])
            return ps_, pk_

        def evac(ps_, pk_, eng):
            t_, kt_ = tmp()
            if eng == "act":
                uop("act", lambda e, t_=t_, ps_=ps_: e.activation(out=t_[:], in_=ps_[:, 0:128], func=AF.Copy), reads=[pk_], writes=[kt_])
            else:
                uop("dve", lambda e, t_=t_, ps_=ps_: e.tensor_copy(out=t_[:], in_=ps_[:, 0:128]), reads=[pk_], writes=[kt_])
            return t_, kt_

        def comb(ps_, pk_, base, kbase, op):
            t_, kt_ = tmp()
            uop("dve", lambda e, t_=t_, ps_=ps_, base=base: e.tensor_tensor(out=t_[:], in0=base[:], in1=ps_[:, 0:128], op=op),
                reads=[pk_, kbase], writes=[kt_])
            return t_, kt_

        def masked(src, ksrc, midx):
            t_, kt_ = tmp()
            uop("dve", lambda e, t_=t_, src=src: e.tensor_tensor(out=t_[:], in0=src[:], in1=bm(midx), op=ALU.mult), reads=[ksrc, "consts"], writes=[kt_])
            return t_, kt_

        yield
        Md, kMd = masked(Nt, kNt, 0)
        Mdt, kMdt = masked(N_, kN, 0)
        ImMd, kImMd = tmp()
        uop("dve", lambda e: e.scalar_tensor_tensor(out=ImMd[:], in0=Md[:], scalar=-1.0, in1=ident, op0=ALU.mult, op1=ALU.add), reads=[kMd, "consts"], writes=[kImMd])
        ImMdt, kImMdt = tmp()
        uop("dve", lambda e: e.scalar_tensor_tensor(out=ImMdt[:], in0=Mdt[:], scalar=-1.0, in1=ident, op0=ALU.mult, op1=ALU.add), reads=[kMdt, "consts"], writes=[kImMdt])
        p_, pk_ = mm_to(Mdt, kMdt, Md, kMd)
        P1, kP1 = evac(p_, pk_, "act")
        yield
        p_, pk_ = mm_to(Md, kMd, Mdt, kMdt)
        P1t, kP1t = evac(p_, pk_, "dve")
        yield
        p_, pk_ = mm_to(ImMdt, kImMdt, P1, kP1)
        Tn, kTn = comb(p_, pk_, ImMd, kImMd, ALU.add)
        yield
        p_, pk_ = mm_to(ImMd, kImMd, P1t, kP1t)
        TT, kTT = comb(p_, pk_, ImMdt, kImMdt, ALU.add)
        yield
        p_, pk_ = mm_to(P1t, kP1t, P1, kP1)
        P2, kP2 = evac(p_, pk_, "act")
        yield
        p_, pk_ = mm_to(P1, kP1, P1t, kP1t)
        P2t, kP2t = evac(p_, pk_, "dve")
        yield
        p_, pk_ = mm_to(TT, kTT, P2, kP2)
        Tn2, kTn2 = comb(p_, pk_, Tn, kTn, ALU.add)
        yield
        p_, pk_ = mm_to(Tn, kTn, P2t, kP2t)
        TT2, kTT2 = comb(p_, pk_, TT, kTT, ALU.add)
        yield
        Tn, kTn, TT, kTT = Tn2, kTn2, TT2, kTT2
        for li in range(4):
            iB = 1 + li if dr == 0 else 5 + li
            iBt = 5 + li if dr == 0 else 1 + li
            Bm, kBm = masked(Nt, kNt, iB)
            Btm, kBtm = masked(N_, kN, iBt)
            p_, pk_ = mm_to(Btm, kBtm, Tn, kTn)
            W_, kW_ = evac(p_, pk_, "act")
            yield
            p_, pk_ = mm_to(Bm, kBm, TT, kTT)
            X_, kX_ = evac(p_, pk_, "dve")
            yield
            p_, pk_ = mm_to(TT, kTT, W_, kW_)
            Tn2, kTn2 = comb(p_, pk_, Tn, kTn, ALU.subtract)
            yield
            p_, pk_ = mm_to(Tn, kTn, X_, kX_)
            TT2, kTT2 = comb(p_, pk_, TT, kTT, ALU.subtract)
            yield
            Tn, kTn, TT, kTT = Tn2, kTn2, TT2, kTT2
        psF, kF = self.next_ps()
        uop("pe", lambda e: e.matmul(psF[:, 0:128], lhsT=TT[:], rhs=Vb[:], start=True, stop=True), reads=[kTT, kVb], writes=[kF])
        uop("pe", lambda e: e.matmul(psF[:, 128:256], lhsT=X[:], rhs=TT[:], start=True, stop=True), reads=[kTT, kX], writes=[kF])
        uop("act", lambda e: e.activation(out=Ust[:, sl, :], in_=psF[:, 0:128], func=AF.Copy), reads=[kF], writes=[("U", sl)])
        uop("dve", lambda e: e.tensor_copy(out=Wst[:, sl, :], in_=psF[:, 128:256]), reads=[kF], writes=[("W", sl)])

    def seq_step(dr, sl, c, Sc, kSc, Sn, kSn):
        psW, kW = self.next_ps()
        P.op("pe", lambda e: e.matmul(psW[:, 0:128], lhsT=Wst[:, sl, :], rhs=Sc[:], start=True, stop=True), reads=[("W", sl), kSc], writes=[kW])
        vn, kvn = tmp()
        P.op("dve", lambda e: e.tensor_tensor(out=vn[:], in0=Ust[:, sl, :], in1=psW[:, 0:128], op=ALU.subtract), reads=[("U", sl), kW], writes=[kvn])
        P.op("pe", lambda e: e.matmul(psW[:, 128:256], lhsT=Qst[:, sl, :], rhs=Sc[:], start=True, stop=False), reads=[("Q", sl), kSc], writes=[kW])
        P.op("pe", lambda e: e.matmul(psW[:, 128:256], lhsT=Ast[:, sl, :], rhs=vn[:], start=False, stop=True), reads=[("A", sl), kvn], writes=[kW])
        if dr == 0:
            P.op("act", lambda e: e.activation(out=Oacc[:, c, :], in_=psW[:, 128:256], func=AF.Copy), reads=[kW], writes=[("O", c)])
        else:
            P.op("dve", lambda e: e.tensor_tensor(out=Oacc[:, c, :], in0=Oacc[:, c, :], in1=psW[:, 128:256], op=ALU.add),
                 reads=[kW, ("O", c)], writes=[("O", c)])
        P.op("pe", lambda e: e.matmul(psW[:, 256:384], lhsT=Kst[:, sl, :], rhs=vn[:], start=True, stop=True), reads=[("K", sl), kvn], writes=[kW])
        P.op("dve", lambda e: e.scalar_tensor_tensor(out=Sn[:], in0=Sc[:], scalar=EGL[:, sl:sl + 1], in1=psW[:, 256:384], op0=ALU.mult, op1=ALU.add),
             reads=[kSc, ("EGL", sl), kW], writes=[kSn])

    okeys = [("O", c) for c in range(NCH)]
    P.op("dve", lambda e: e.memset(Oflat, 0.0), writes=okeys + ["Oflat"])
    import os as _os
    DBG_QH = int(_os.environ.get("GDN_QH", "16"))
    DBG_ST = int(_os.environ.get("GDN_ST", "9"))
    for qh in range(DBG_QH):
        P.op("dve", lambda e: e.tensor_copy(out=Oflat[:, 0:1], in_=Oflat[:, 0:1]), reads=okeys + ["Oflat"], writes=["Oflat"])
        conv_silu_row(self, QT, "QT", Oflat, "Oflat", pT[qh * 128:(qh + 1) * 128, :], gcol(qh))
        l2norm_row(QT, "QT", 128 ** -0.5)
        conv_silu_row(self, KT, "KT", Oflat, "Oflat", pT[D + qh * 128:D + (qh + 1) * 128, :], gcol(16 + qh))
        l2norm_row(KT, "KT", 1.0)
        for vh in (2 * qh, 2 * qh + 1):
            P.op("dve", lambda e: e.tensor_copy(out=Oflat[:, 0:1], in_=Oflat[:, 0:1]), reads=okeys + ["Oflat"], writes=["Oflat"])
            conv_silu_row(self, VT, "VT", Oflat, "Oflat", pT[2 * D + vh * 128:2 * D + (vh + 1) * 128, :], gcol(32 + vh))
            P.dma("sp", Oflat, pT[4 * D + vh * 128:4 * D + (vh + 1) * 128, :], reads=["Oflat"], writes=["Oflat"])
            P.op("act", lambda e: e.activation(out=SZ[:], in_=Oflat, func=AF.Silu), reads=["Oflat"], writes=["SZ"])
            P.op("dve", lambda e: e.tensor_copy(out=Oflat[:, 0:1], in_=Oflat[:, 0:1]), reads=["Oflat"], writes=okeys + ["Oflat"])
            for dr in range(2 if DBG_ST >= 3 else 0):
                order = order_f if dr == 0 else order_b
                order = order[:int(_os2.environ.get("GDN_NCH", "34"))]
                P.op("dve", lambda e: e.memset(Sst[0][:], 0.0), writes=["S0"])
                si = 0
                for g0 in range(0, len(order), HC):
                    grp = order[g0:g0 + HC]
                    pending = [unit(vh, dr, sl, c) for sl, c in enumerate(grp)]
                    active = []
                    while pending or active:
                        while len(active) < 2 and pending:
                            active.append(pending.pop(0))
                        for g_ in list(active):
                            try:
                                next(g_)
                            except StopIteration:
                                active.remove(g_)
                    for sl, c in enumerate(grp):
                        if DBG_ST >= 4:
                            seq_step(dr, sl, c, Sst[si % 2], f"S{si % 2}", Sst[(si + 1) % 2], f"S{(si + 1) % 2}")
                        si += 1
            for c in range(NCH if DBG_ST >= 5 else 0):
                junk, kj = tmp()
                P.op("act", lambda e, junk=junk, c=c: e.activation(out=junk[:], in_=Oacc[:, c, :], func=AF.Square),
                     reads=[("O", c)], writes=[kj])
                P.op("dve", lambda e, junk=junk, c=c: e.reduce_sum(out=ss[:, c:c + 1], in_=junk[:], axis=mybir.AxisListType.X),
                     reads=[kj], writes=["ss"])
            if DBG_ST >= 5:
                P.op("act", lambda e: e.activation(out=ss[:], in_=ss[:], func=AF.Ln, bias=self.eps_col[:], scale=1.0 / 128), reads=["ss", "eps"], writes=["ss"])
                P.op("act", lambda e: e.activation(out=ss[:], in_=ss[:], func=AF.Exp, scale=-0.5), reads=["ss"], writes=["ss"])
            for c in range(NCH if DBG_ST >= 5 else 0):
                On, kOn = tmp()
                P.op("dve", lambda e, On=On, c=c: e.tensor_scalar(out=On[:], in0=Oacc[:, c, :], scalar1=ss[:, c:c + 1], scalar2=None, op0=ALU.mult),
                     reads=[("O", c), "ss"], writes=[kOn])
                ps, pk = self.next_ps()
                P.op("pe", lambda e, ps=ps, On=On: e.transpose(out=ps[:, 0:128], in_=On[:], identity=ident), reads=[kOn, "consts"], writes=[pk])
                if col and c >= 2:
                    lc = c - 2
                    o_ap = yrow[:, NCTX:].rearrange("p (r w) -> p w r", w=64)[:, 2 * lc:2 * lc + 2, :]
                    i0 = ps[:, 0:128].rearrange("p (w r) -> p w r", r=64)
                    i1 = SZ[:, c * 128:(c + 1) * 128].rearrange("p (w r) -> p w r", r=64)
                else:
                    o_ap = yrow[:, c * 128:(c + 1) * 128]
                    i0 = ps[:, 0:128]
                    i1 = SZ[:, c * 128:(c + 1) * 128]
                P.op("dve", lambda e, o_ap=o_ap, i0=i0, i1=i1: e.scalar_tensor_tensor(
                    out=o_ap, in0=i0, scalar=self.cs("gnorm", lj), in1=i1, op0=ALU.mult, op1=ALU.mult),
                    reads=[pk, "SZ", "consts"], writes=["yrow"])
            if DBG_ST >= 5:
                P.dma("sp", yT[vh * 128:(vh + 1) * 128, :], yrow[:], reads=["yrow"])
    P.pop()


Builder.gdn_mixer = gdn_mixer


def build_full(stop=None):
    B = Builder()
    P = B.P
    B.setup()
    xT = B.ext_in("xT", [D, S])
    outT = B.ext_out("outT", [D, NLAT]) if stop is None else None
    resdump = B.ext_out("resdump", [D, S]) if stop is not None else None
    w_mod = B.ext_in("w_mod", [DEPTH, D, 6 * D])
    gdn_w_in = B.ext_in("gdn_w_in", [2, D, GDN_IN])
    gdn_w_out = B.ext_in("gdn_w_out", [2, 2 * D, D])
    ml_w_in = B.ext_in("ml_w_in", [1, D, ML_IN])
    ml_w_out = B.ext_in("ml_w_out", [1, D, D])
    lru_w_in = B.ext_in("lru_w_in", [1, D, LRU_IN])
    lru_w_gate = B.ext_in("lru_w_gate", [1, 2, 2, 8, 256, 256])
    lru_w_out = B.ext_in("lru_w_out", [1, D, D])
    ff_w_up = B.ext_in("ff_w_up", [DEPTH, D, DFF])
    ff_w_down = B.ext_in("ff_w_down", [DEPTH, DFF, D])
    res = P.dram("resT", [D, S])
    pT = P.dram("pT", [GDN_IN, S])
    yT = P.dram("yT", [2 * D, S], BF16)
    aT = P.dram("aT", [DFF, S], BF16)
    P.dma("sp", res, xT)
    P.barrier()
    B.modulation(w_mod)
    for i in range(DEPTH):
        kind, j = i % 3, i // 3
        col = (i % 2 == 1)
        P.push()
        hT = P.sbuf("hT", [128, 16, S], BF16)
        B.norm_mod(res, lambda k, s, i=i: B.Amod[:, i, 0, k, s:s + 1], lambda k, s, i=i: B.mod_ap(i, 0, k, s), hT=hT)
        G_fn = lambda d, s, i=i: B.mod_ap(i, 2, d, s)
        if kind == 0:
            B.in_proj(gdn_w_in[j], GDN_IN, hT, pT, col)
            P.pop()
            B.gdn_mixer(j, pT, yT, col)
            B.out_proj(yT, gdn_w_out[j], 2 * D, G_fn, res)
        elif kind == 1:
            B.in_proj(ml_w_in[j], ML_IN, hT, pT[0:ML_IN, :], col)
            P.pop()
            B.mlstm_mixer(pT[0:ML_IN, :], yT[0:D, :], col)
            B.out_proj(yT[0:D, :], ml_w_out[j], D, G_fn, res)
        else:
            B.in_proj(lru_w_in[j], LRU_IN, hT, pT[0:LRU_IN, :], col)
            P.pop()
            B.lru_mixer(pT[0:LRU_IN, :], yT[0:D, :], lru_w_gate[j])
            B.out_proj(yT[0:D, :], lru_w_out[j], D, G_fn, res)
        if stop == (i, "mix"):
            P.barrier()
            P.dma("sp", resdump, res)
            return P.finish()
        B.ffn(i, ff_w_up[i], ff_w_down[i], res, aT)
        if stop == (i, "ffn"):
            P.barrier()
            P.dma("sp", resdump, res)
            return P.finish()
    B.norm_mod(res, lambda k, s: B.cs("nout", k), lambda k, s: B.zero_col[:], dst=outT, dst_off=NCTX)
    return P.finish()


def kernel(**inputs):
    inp = {k: np.asarray(v) for k, v in inputs.items()}
    nc = build_full()
    in_maps = []
    NCORE = 4
    for core in range(NCORE):
        b = core % 4
        xt = np.ascontiguousarray(np.concatenate([inp["ctx"][b], inp["x"][b]], axis=0).T.astype(np.float32))
        m = {"consts": make_consts(inp, b), "xT": xt}
        for k in ("w_mod", "gdn_w_in", "gdn_w_out", "ml_w_in", "ml_w_out", "lru_w_in", "lru_w_gate", "lru_w_out",
                  "ff_w_up", "ff_w_down"):
            m[k] = np.ascontiguousarray(inp[k], dtype=np.float32)
        in_maps.append(m)
    res = run_bass_kernel_spmd(nc, in_maps, core_ids=list(range(NCORE)))
    out = np.stack([np.ascontiguousarray(res.results[b]["outT"].T) for b in range(4)], axis=0)
    return out.astype(np.float32)


def build_gdn_only(col=False, lj=0):
    B = Builder()
    P = B.P
    B.setup()
    pT = B.ext_in("pT", [GDN_IN, S])
    yT = B.ext_out("yT", [2 * D, S], BF16)
    P.barrier()
    B.gdn_mixer(lj, pT, yT, col)
    return P.finish()


def mlstm_mixer(self, pT, yT, col):
    P = self.P
    P.push()
    NCH = 34
    ident = self.cs("ident", 0, 128)

    def mask_ap(dr, strict):
        return self.cs("mask", (dr * 2 + strict) * 128, 128)

    onesf = P.sbuf("onesf", [128, 128])
    P.op("dve", lambda e: e.memset(onesf[:], 1.0), writes=["onesf"])
    GT = [P.sbuf("mGT", [128, NCH, 32]) for _ in range(2)]
    NB = [P.sbuf("mNB", [128, NCH, 32]) for _ in range(2)]
    P.push()
    GR = P.sbuf("GR", [32, S])
    LF = P.sbuf("LF", [32, S])
    b15 = P.sbuf("b15", [32, 1])
    tri = [P.sbuf("tri", [128, 128]) for _ in range(2)]
    gtb = [P.sbuf("gtb", [128, 64]) for _ in range(2)]
    for dr in range(2):
        P.op("dve", lambda e, dr=dr: e.tensor_scalar(out=tri[dr][:], in0=mask_ap(dr, 0), scalar1=1.0 / 30000.0, scalar2=1.0,
                                                     op0=ALU.mult, op1=ALU.add), reads=["consts"], writes=[f"tri{dr}"])
    P.dma("sp", GR[:], pT[6144:6176, :], writes=["GR"])
    P.op("dve", lambda e: e.tensor_scalar(out=b15[:], in0=self.cs("mgb", 0)[0:32, :], scalar1=1.0 / 15.0, scalar2=None, op0=ALU.mult),
         reads=["consts"], writes=["b15"])
    W1 = P.sbuf("W1", [32, S])
    W2 = P.sbuf("W2", [32, S])
    W3 = P.sbuf("W3", [32, S])

    def dv(fn, reads, writes):
        P.op("dve", fn, reads=reads, writes=writes)

    P.op("dve", lambda e: e.tensor_scalar(out=b15[:], in0=b15[:], scalar1=2.0, scalar2=None, op0=ALU.mult), reads=["b15"], writes=["b15"])
    P.op("act", lambda e: e.activation(out=W1[:], in_=GR[:], func=AF.Exp, bias=b15[:], scale=2.0 / 15.0), reads=["GR", "b15"], writes=["W1"])
    dv(lambda e: e.tensor_scalar(out=W2[:], in0=W1[:], scalar1=1.0, scalar2=None, op0=ALU.add), ["W1"], ["W2"])
    dv(lambda e: e.reciprocal(out=W2[:], in_=W2[:]), ["W2"], ["W2"])
    dv(lambda e: e.tensor_scalar(out=W1[:], in0=W1[:], scalar1=-1.0, scalar2=15.0, op0=ALU.add, op1=ALU.mult), ["W1"], ["W1"])
    dv(lambda e: e.tensor_tensor(out=GR[:], in0=W1[:], in1=W2[:], op=ALU.mult), ["W1", "W2"], ["GR"])
    dv(lambda e: e.tensor_scalar(out=W1[:], in0=GR[:], scalar1=-1.0, scalar2=None, op0=ALU.mult), ["GR"], ["W1"])
    dv(lambda e: e.tensor_tensor(out=W2[:], in0=W1[:], in1=GR[:], op=ALU.max), ["W1", "GR"], ["W2"])
    P.op("act", lambda e: e.activation(out=W2[:], in_=W2[:], func=AF.Exp, scale=-1.0), reads=["W2"], writes=["W2"])
    dv(lambda e: e.tensor_scalar(out=W3[:], in0=W2[:], scalar1=2.0, scalar2=None, op0=ALU.add), ["W2"], ["W3"])
    dv(lambda e: e.reciprocal(out=W3[:], in_=W3[:]), ["W3"], ["W3"])
    dv(lambda e: e.tensor_tensor(out=W2[:], in0=W2[:], in1=W3[:], op=ALU.mult), ["W2", "W3"], ["W2"])
    dv(lambda e: e.tensor_tensor(out=W3[:], in0=W2[:], in1=W2[:], op=ALU.mult), ["W2"], ["W3"])
    dv(lambda e: e.tensor_scalar(out=LF[:], in0=W3[:], scalar1=1.0 / 11.0, scalar2=1.0 / 9.0, op0=ALU.mult, op1=ALU.add), ["W3"], ["LF"])
    for cst in (1.0 / 7.0, 1.0 / 5.0, 1.0 / 3.0, 1.0):
        dv(lambda e: e.tensor_tensor(out=LF[:], in0=LF[:], in1=W3[:], op=ALU.mult), ["LF", "W3"], ["LF"])
        dv(lambda e, cst=cst: e.tensor_scalar(out=LF[:], in0=LF[:], scalar1=cst, scalar2=None, op0=ALU.add), ["LF"], ["LF"])
    dv(lambda e: e.tensor_tensor(out=LF[:], in0=LF[:], in1=W2[:], op=ALU.mult), ["LF", "W2"], ["LF"])
    dv(lambda e: e.tensor_scalar(out=W1[:], in0=W1[:], scalar1=0.0, scalar2=None, op0=ALU.max), ["W1"], ["W1"])
    dv(lambda e: e.scalar_tensor_tensor(out=LF[:], in0=LF[:], scalar=2.0, in1=W1[:], op0=ALU.mult, op1=ALU.add), ["LF", "W1"], ["LF"])
    dv(lambda e: e.tensor_scalar(out=LF[:], in0=LF[:], scalar1=-1.0, scalar2=None, op0=ALU.mult), ["LF"], ["LF"])
    for c in range(NCH):
        cs_ = slice(c * 128, (c + 1) * 128)
        gt_, kgt = gtb[c % 2], f"gtb{c % 2}"
        ps, pk = self.next_ps()
        P.op("pe", lambda e, ps=ps, cs_=cs_: e.transpose(out=ps[:, 0:32], in_=GR[:, cs_], identity=ident[0:32, 0:32]), reads=["GR", "consts"], writes=[pk])
        P.op("pe", lambda e, ps=ps, cs_=cs_: e.transpose(out=ps[:, 32:64], in_=LF[:, cs_], identity=ident[0:32, 0:32]), reads=["LF", "consts"], writes=[pk])
        P.op("act", lambda e, ps=ps, gt_=gt_: e.activation(out=gt_[:], in_=ps[:, 0:64], func=AF.Copy), reads=[pk], writes=[kgt])
        ps2, pk2 = self.next_ps()
        P.op("pe", lambda e, ps2=ps2, gt_=gt_: e.matmul(ps2[:, 0:32], lhsT=tri[0][:], rhs=gt_[:, 32:64], start=True, stop=True), reads=["tri0", kgt], writes=[pk2])
        P.op("pe", lambda e, ps2=ps2, gt_=gt_: e.matmul(ps2[:, 32:64], lhsT=tri[1][:], rhs=gt_[:, 32:64], start=True, stop=True), reads=["tri1", kgt], writes=[pk2])
        for dr in range(2):
            P.op("dve", lambda e, ps2=ps2, c=c, dr=dr: e.tensor_copy(out=GT[dr][:, c, :], in_=ps2[:, dr * 32:(dr + 1) * 32]), reads=[pk2], writes=[("mGT", dr)])
            P.op("dve", lambda e, gt_=gt_, c=c, dr=dr: e.tensor_tensor(out=NB[dr][:, c, 0:24], in0=gt_[:, 0:24], in1=GT[dr][:, c, 8:32], op=ALU.subtract),
                 reads=[kgt, ("mGT", dr)], writes=[("mNB", dr)])
    P.pop()
    QT = P.sbuf("QT", [128, S])
    KT = P.sbuf("KT", [128, S])
    Vt = P.sbuf("Vt", [128, NCH, 257])
    Hacc = P.sbuf("Hacc", [128, NCH, 256])
    Hflat = Hacc[:].rearrange("p c e -> p (c e)")
    SO = P.sbuf("SO", [128, 2, S], BF16)
    HC = 9
    Ast = P.sbuf("Ast", [128, HC, 128])
    Kst = P.sbuf("Kst", [128, HC, 128])
    Qst = P.sbuf("Qst", [128, HC, 128])
    EGL = P.sbuf("EGL", [128, HC])
    yrow = [P.sbuf("yrow", [128, S], BF16) for _ in range(2)]
    Cst = [P.sbuf("Cst", [128, 257]) for _ in range(2)]
    NT = 16
    tmps = [P.sbuf("tm", [128, 128]) for _ in range(NT)]
    tcnt = [0]

    def tmp():
        i = tcnt[0] % NT
        tcnt[0] += 1
        return tmps[i], f"tm{i}"

    cols_ = [P.sbuf("cl", [128, 1]) for _ in range(12)]
    ccnt = [0]

    def colt():
        i = ccnt[0] % 12
        ccnt[0] += 1
        return cols_[i], f"cl{i}"

    hn = [P.sbuf("hn", [128, 256]) for _ in range(2)]
    ss = P.sbuf("ss", [128, NCH])
    hkeys = [("H", c) for c in range(NCH)]
    P.op("dve", lambda e: e.memset(Hflat, 0.0), writes=hkeys + ["Hflat"])
    P.op("dve", lambda e: e.memset(Vt[:, :, 256:257], 1.0), writes=["Vt1"])
    order_f = list(range(NCH))
    order_b = [1, 0] + list(range(NCH - 1, 1, -1))

    def unit(hd, dr, sl, c):
        rf = dr * 16 + 8 + hd
        ri = dr * 16 + hd
        cs_ = slice(c * 128, (c + 1) * 128)
        last = 127 if dr == 0 else 0
        dg, kdg = tmp()
        P.op("dve", lambda e: e.tensor_scalar(out=dg[:], in0=ident, scalar1=GT[dr][:, c, rf:rf + 1], scalar2=None, op0=ALU.mult),
             reads=["consts", ("mGT", dr)], writes=[kdg])
        psA, kA = self.next_ps()
        P.op("pe", lambda e: e.matmul(psA[:, 0:128], lhsT=onesf[:], rhs=dg[:], start=True, stop=False), reads=[kdg, "onesf"], writes=[kA])
        P.op("pe", lambda e: e.matmul(psA[:, 0:128], lhsT=ident, rhs=mask_ap(dr, 0), start=False, stop=True), reads=["consts"], writes=[kA])
        P.op("pe", lambda e: e.matmul(psA[:, 128:256], lhsT=onesf[:], rhs=dg[:], start=True, stop=True), reads=[kdg, "onesf"], writes=[kA])
        nbc = NB[dr][:, c, ri:ri + 1]
        Dm, kDm = tmp()
        P.op("act", lambda e: e.activation(out=Dm[:], in_=psA[:, 0:128], func=AF.Exp, bias=nbc, scale=1.0), reads=[kA, ("mNB", dr)], writes=[kDm])
        EG, kEG = tmp()
        P.op("act", lambda e: e.activation(out=EG[:], in_=psA[:, 128:256], func=AF.Exp), reads=[kA], writes=[kEG])
        glc, kglc = colt()
        P.op("act", lambda e: e.activation(out=glc[:], in_=psA[:, 128 + last:128 + last + 1], func=AF.Copy), reads=[kA], writes=[kglc])
        kdc, kkdc = colt()
        P.op("act", lambda e: e.activation(out=kdc[:], in_=nbc, func=AF.Exp, bias=glc[:], scale=1.0), reads=[kglc, ("mNB", dr)], writes=[kkdc])
        P.op("dve", lambda e: e.tensor_tensor(out=Qst[:, sl, :], in0=QT[:, cs_], in1=EG[:], op=ALU.mult), reads=["QT", kEG], writes=[("Q", sl)])
        P.op("dve", lambda e: e.tensor_copy(out=EGL[:, sl:sl + 1], in_=EG[:, last:last + 1]), reads=[kEG], writes=[("EGL", sl)])
        psK, kK = self.next_ps()
        P.op("pe", lambda e: e.matmul(psK[:, 0:128], lhsT=KT[:, cs_], rhs=QT[:, cs_], start=True, stop=True), reads=["KT", "QT"], writes=[kK])
        P.op("pe", lambda e: e.transpose(out=psK[:, 128:256], in_=KT[:, cs_], identity=ident), reads=["KT", "consts"], writes=[kK])
        P.op("dve", lambda e: e.tensor_tensor(out=Ast[:, sl, :], in0=psK[:, 0:128], in1=Dm[:], op=ALU.mult), reads=[kK, kDm], writes=[("A", sl)])
        P.op("dve", lambda e: e.tensor_scalar(out=Kst[:, sl, :], in0=psK[:, 128:256], scalar1=kdc[:, 0:1], scalar2=None, op0=ALU.mult),
             reads=[kK, kkdc], writes=[("K", sl)])

    def seq_step(dr, sl, c, Cc, kCc, Cn, kCn):
        psO, kO = self.next_ps()
        P.op("pe", lambda e: e.matmul(psO[:, 0:257], lhsT=Qst[:, sl, :], rhs=Cc[:], start=True, stop=False), reads=[("Q", sl), kCc], writes=[kO])
        P.op("pe", lambda e: e.matmul(psO[:, 0:257], lhsT=Ast[:, sl, :], rhs=Vt[:, c, :], start=False, stop=True), reads=[("A", sl), "Vt", "Vt1"], writes=[kO])
        dd, kdd = colt()
        P.op("dve", lambda e: e.tensor_scalar(out=dd[:], in0=psO[:, 256:257], scalar1=-1.0, scalar2=1.0, op0=ALU.mult, op1=ALU.max), reads=[kO], writes=[kdd])
        P.op("dve", lambda e: e.tensor_tensor(out=dd[:], in0=dd[:], in1=psO[:, 256:257], op=ALU.max), reads=[kO, kdd], writes=[kdd])
        P.op("dve", lambda e: e.reciprocal(out=dd[:], in_=dd[:]), reads=[kdd], writes=[kdd])
        P.op("dve", lambda e: e.scalar_tensor_tensor(out=Hacc[:, c, :], in0=psO[:, 0:256], scalar=dd[:, 0:1], in1=Hacc[:, c, :], op0=ALU.mult, op1=ALU.add),
             reads=[kO, kdd, ("H", c)], writes=[("H", c)])
        psS, kS = self.next_ps()
        P.op("pe", lambda e: e.matmul(psS[:, 0:257], lhsT=Kst[:, sl, :], rhs=Vt[:, c, :], start=True, stop=True), reads=[("K", sl), "Vt", "Vt1"], writes=[kS])
        P.op("dve", lambda e: e.scalar_tensor_tensor(out=Cn[:], in0=Cc[:], scalar=EGL[:, sl:sl + 1], in1=psS[:, 0:257], op0=ALU.mult, op1=ALU.add),
             reads=[kCc, ("EGL", sl), kS], writes=[kCn])

    import os as _os3
    for hd in range(int(_os3.environ.get("ML_H", "8"))):
        P.dma("sp", QT[:], pT[hd * 128:(hd + 1) * 128, :], writes=["QT"])
        P.op("dve", lambda e: e.tensor_scalar(out=QT[:], in0=QT[:], scalar1=128 ** -0.5, scalar2=None, op0=ALU.mult), reads=["QT"], writes=["QT"])
        P.dma("sp", KT[:], pT[1024 + hd * 128:1024 + (hd + 1) * 128, :], writes=["KT"])
        P.op("dve", lambda e: e.tensor_copy(out=Hflat[:, 0:1], in_=Hflat[:, 0:1]), reads=hkeys + ["Hflat"], writes=["Hflat"])
        P.dma("sp", Hflat[:, 0:S], pT[2048 + hd * 256:2048 + hd * 256 + 128, :], reads=["Hflat"], writes=["Hflat"])
        P.dma("sp", Hflat[:, S:2 * S], pT[2048 + hd * 256 + 128:2048 + (hd + 1) * 256, :], reads=["Hflat"], writes=["Hflat"])
        for c in range(NCH):
            ps, pk = self.next_ps()
            for hf in range(2):
                P.op("pe", lambda e, ps=ps, c=c, hf=hf: e.transpose(out=ps[:, hf * 128:(hf + 1) * 128], in_=Hflat[:, hf * S + c * 128:hf * S + (c + 1) * 128], identity=ident),
                     reads=["Hflat", "consts"], writes=[pk])
            self.evac_copy(self.alt_eng(), Vt[:, c, 0:256], ps[:, 0:256], [pk], ["Vt"])
        for hf in range(2):
            P.dma("sp", Hflat[:, 0:S], pT[4096 + hd * 256 + hf * 128:4096 + hd * 256 + (hf + 1) * 128, :], reads=["Hflat", "Vt"], writes=["Hflat"])
            P.op("act", lambda e, hf=hf: e.activation(out=SO[:, hf, :], in_=Hflat[:, 0:S], func=AF.Sigmoid), reads=["Hflat"], writes=["SO"])
        P.op("dve", lambda e: e.memset(Hflat, 0.0), reads=["Hflat"], writes=hkeys + ["Hflat"])
        for dr in range(2):
            order = order_f if dr == 0 else order_b
            P.op("dve", lambda e: e.memset(Cst[0][:], 0.0), writes=["C0"])
            si = 0
            for g0 in range(0, NCH, HC):
                grp = order[g0:g0 + HC]
                for sl, c in enumerate(grp):
                    unit(hd, dr, sl, c)
                for sl, c in enumerate(grp):
                    seq_step(dr, sl, c, Cst[si % 2], f"C{si % 2}", Cst[(si + 1) % 2], f"C{(si + 1) % 2}")
                    si += 1
        for c in range(NCH):
            h_, kh = hn[c % 2], f"hn{c % 2}"
            P.op("act", lambda e, h_=h_, c=c: e.activation(out=h_[:], in_=Hacc[:, c, :], func=AF.Square), reads=[("H", c)], writes=[kh])
            P.op("dve", lambda e, h_=h_, c=c: e.reduce_sum(out=ss[:, c:c + 1], in_=h_[:], axis=mybir.AxisListType.X), reads=[kh], writes=["ss"])
        P.op("act", lambda e: e.activation(out=ss[:], in_=ss[:], func=AF.Ln, bias=self.eps_col[:], scale=1.0 / 256), reads=["ss", "eps"], writes=["ss"])
        P.op("act", lambda e: e.activation(out=ss[:], in_=ss[:], func=AF.Exp, scale=-0.5), reads=["ss"], writes=["ss"])
        for c in range(NCH):
            h_, kh = hn[c % 2], f"hn{c % 2}"
            P.op("dve", lambda e, h_=h_, c=c: e.tensor_scalar(out=h_[:], in0=Hacc[:, c, :], scalar1=ss[:, c:c + 1], scalar2=None, op0=ALU.mult),
                 reads=[("H", c), "ss"], writes=[kh])
            for hf in range(2):
                ps, pk = self.next_ps()
                P.op("pe", lambda e, ps=ps, h_=h_, hf=hf: e.transpose(out=ps[:, 0:128], in_=h_[:, hf * 128:(hf + 1) * 128], identity=ident),
                     reads=[kh, "consts"], writes=[pk])
                if col and c >= 2:
                    lc = c - 2
                    o_ap = yrow[hf][:, NCTX:].rearrange("p (r w) -> p w r", w=64)[:, 2 * lc:2 * lc + 2, :]
                    i0 = ps[:, 0:128].rearrange("p (w r) -> p w r", r=64)
                    i1 = SO[:, hf, c * 128:(c + 1) * 128].rearrange("p (w r) -> p w r", r=64)
                else:
                    o_ap = yrow[hf][:, c * 128:(c + 1) * 128]
                    i0 = ps[:, 0:128]
                    i1 = SO[:, hf, c * 128:(c + 1) * 128]
                mn_ap = self.cs("mnorm", hd * 2 + hf)
                P.op("dve", lambda e, o_ap=o_ap, i0=i0, i1=i1, mn_ap=mn_ap: e.scalar_tensor_tensor(
                    out=o_ap, in0=i0, scalar=mn_ap, in1=i1, op0=ALU.mult, op1=ALU.mult),
                    reads=[pk, "SO", "consts"], writes=[f"yrow{hf}"])
        for hf in range(2):
            P.dma("sp", yT[hd * 256 + hf * 128:hd * 256 + (hf + 1) * 128, :], yrow[hf][:], reads=[f"yrow{hf}"])
    P.pop()


Builder.mlstm_mixer = mlstm_mixer


def build_ml_only(col=False):
    B = Builder()
    P = B.P
    B.setup()
    pT = B.ext_in("pT", [ML_IN, S])
    yT = B.ext_out("yT", [D, S], BF16)
    P.barrier()
    B.mlstm_mixer(pT, yT, col)
    return P.finish()
```
